# Optimizing a Trainium2 kernel written in Bass

```python
import math
import jax, jax.numpy as jnp
from jax import lax
import numpy as np

D_MODEL = 2048
BATCH = 2
SEQ = 8192
DEPTH = 1

HEAD_DIM = 128
BLOCK = 128
EPS = 1e-6
DIL_PATTERNS = ((128, 1), (512, 4), (2048, 16))
N_GROUPS = len(DIL_PATTERNS)
A_HEADS_PER_GROUP = 8
A_HEADS = N_GROUPS * A_HEADS_PER_GROUP
A_WIDTH = A_HEADS * HEAD_DIM
A_OUT = A_HEADS_PER_GROUP * HEAD_DIM
B_Q_HEADS = 16
B_KV_HEADS = 2
B_GROUP = B_Q_HEADS // B_KV_HEADS
B_WINDOW = 128
B_Q_WIDTH = B_Q_HEADS * HEAD_DIM
B_KV_WIDTH = B_KV_HEADS * HEAD_DIM
D_FF = 4 * D_MODEL
N_ATTN_HEADS = B_Q_HEADS + A_HEADS
OFF_QA = 0
OFF_KA = OFF_QA + A_WIDTH
OFF_VA = OFF_KA + A_WIDTH
OFF_QB = OFF_VA + A_WIDTH
OFF_KB = OFF_QB + B_Q_WIDTH
OFF_VB = OFF_KB + B_KV_WIDTH
OFF_GA = OFF_VB + B_KV_WIDTH
OFF_GB = OFF_GA + D_MODEL
IN_COLS = OFF_GB + D_MODEL

kernel_name = "hybrid_dilated_swa_sink_gated_block"


def alibi_slopes():
    i = np.arange(1, N_ATTN_HEADS + 1, dtype=np.float32)
    return (2.0 ** (-8.0 * i / N_ATTN_HEADS)).astype(np.float32)


def rmsnorm(x, g):
    xf = x.astype(jnp.float32)
    y = xf * lax.rsqrt(jnp.mean(xf * xf, axis=-1, keepdims=True) + EPS)
    return (y * g.astype(jnp.float32)).astype(x.dtype)


def banded_attention(q, k, v, slopes, stride, max_back, sink=None):
    N, L0, Hk, G, D = q.shape
    pad = (-L0) % BLOCK
    if pad:
        q = jnp.pad(q, ((0, 0), (0, pad), (0, 0), (0, 0), (0, 0)))
        k = jnp.pad(k, ((0, 0), (0, pad), (0, 0), (0, 0)))
        v = jnp.pad(v, ((0, 0), (0, pad), (0, 0), (0, 0)))
    L = L0 + pad
    nb = L // BLOCK
    qb = q.reshape(N, nb, BLOCK, Hk, G, D)

    def band(a):
        cur = a.reshape(N, nb, BLOCK, Hk, D)
        prev = jnp.pad(cur, ((0, 0), (1, 0), (0, 0), (0, 0), (0, 0)))[:, :-1]
        return jnp.concatenate([prev, cur], axis=2)

    kb, vb = band(k), band(v)
    s = jnp.einsum('nbqhgd,nbkhd->nbhgqk', qb, kb).astype(jnp.float32) * (HEAD_DIM ** -0.5)
    qi = np.arange(BLOCK) + BLOCK
    ki = np.arange(2 * BLOCK)
    rel = qi[:, None] - ki[None, :]
    kpos = np.arange(nb)[:, None, None] * BLOCK - BLOCK + ki[None, None, :]
    valid = ((rel >= 0) & (rel <= max_back))[None] & (kpos >= 0)
    bias = -slopes.astype(jnp.float32)[:, :, None, None] * jnp.asarray(stride * rel, jnp.float32)
    s = jnp.where(jnp.asarray(valid)[None, :, None, None], s + bias[None, None], -jnp.inf)
    m = jnp.max(s, axis=-1)
    if sink is not None:
        sink_f = sink.astype(jnp.float32)[None, None, :, :, None]
        m = jnp.maximum(m, sink_f)
    p = jnp.exp(s - m[..., None])
    denom = jnp.sum(p, axis=-1)
    if sink is not None:
        denom = denom + jnp.exp(sink_f - m)
    o = jnp.einsum('nbhgqk,nbkhd->nbqhgd', p.astype(v.dtype), vb).astype(jnp.float32)
    denom_q = jnp.moveaxis(denom, -1, 2)
    o = o / denom_q[..., None]
    lse = jnp.moveaxis(m, -1, 2) + jnp.log(denom_q)
    o = o.reshape(N, L, Hk, G, D)[:, :L0]
    lse = lse.reshape(N, L, Hk, G)[:, :L0]
    return o, lse


def dilated_group(q, k, v, slopes, window, dilation):
    Bn, T, H, D = q.shape
    Ls = T // dilation

    def gather(a):
        return a.reshape(Bn, Ls, dilation, H, D).transpose(0, 2, 1, 3, 4).reshape(Bn * dilation, Ls, H, D)

    o, lse = banded_attention(gather(q)[:, :, :, None, :], gather(k), gather(v),
                              slopes[:, None], dilation, window // dilation)
    o = o.reshape(Bn, dilation, Ls, H, D).transpose(0, 2, 1, 3, 4).reshape(Bn, T, H, D)
    lse = lse.reshape(Bn, dilation, Ls, H).transpose(0, 2, 1, 3).reshape(Bn, T, H)
    return o, lse


def setup_inputs(seed: int = 0) -> dict:
    key = jax.random.key(seed)
    ks = jax.random.split(key, 15)
    f32 = jnp.float32

    def w(k, shape, fan_in):
        return jax.random.normal(k, shape, f32) * (fan_in ** -0.5)

    def gain(k, shape):
        return 1.0 + 0.1 * jax.random.normal(k, shape, f32)

    return {
        "x": jax.random.normal(ks[0], (BATCH, SEQ, D_MODEL), f32),
        "norm1_g": gain(ks[1], (DEPTH, D_MODEL)),
        "w_in": w(ks[2], (DEPTH, D_MODEL, IN_COLS), D_MODEL),
        "q_norm_a": gain(ks[3], (DEPTH, HEAD_DIM)),
        "k_norm_a": gain(ks[4], (DEPTH, HEAD_DIM)),
        "q_norm_b": gain(ks[5], (DEPTH, HEAD_DIM)),
        "k_norm_b": gain(ks[6], (DEPTH, HEAD_DIM)),
        "sinks_b": 0.5 * jax.random.normal(ks[7], (DEPTH, B_Q_HEADS), f32),
        "w_branch_a": w(ks[8], (DEPTH, A_OUT, D_MODEL), A_OUT),
        "w_branch_b": w(ks[9], (DEPTH, B_Q_WIDTH, D_MODEL), B_Q_WIDTH),
        "w_out": w(ks[10], (DEPTH, D_MODEL, D_MODEL), D_MODEL),
        "norm2_g": gain(ks[11], (DEPTH, D_MODEL)),
        "w_ff1": w(ks[12], (DEPTH, D_MODEL, D_FF), D_MODEL),
        "w_ff2": w(ks[13], (DEPTH, D_FF, D_MODEL), D_FF),
    }


def reference(x, norm1_g, w_in, q_norm_a, k_norm_a, q_norm_b, k_norm_b, sinks_b,
              w_branch_a, w_branch_b, w_out, norm2_g, w_ff1, w_ff2):
    Bn, T, _ = x.shape
    slopes = jnp.asarray(alibi_slopes())
    slopes_b = slopes[:B_Q_HEADS].reshape(B_KV_HEADS, B_GROUP)
    slopes_a = slopes[B_Q_HEADS:].reshape(N_GROUPS, A_HEADS_PER_GROUP)
    for l in range(DEPTH):
        h = rmsnorm(x, norm1_g[l])
        proj = h @ w_in[l]
        shp_a = (Bn, T, N_GROUPS, A_HEADS_PER_GROUP, HEAD_DIM)
        qa = rmsnorm(proj[..., OFF_QA:OFF_KA].reshape(shp_a), q_norm_a[l])
        ka = rmsnorm(proj[..., OFF_KA:OFF_VA].reshape(shp_a), k_norm_a[l])
        va = proj[..., OFF_VA:OFF_QB].reshape(shp_a)
        outs, lses = [], []
        for g, (window, dilation) in enumerate(DIL_PATTERNS):
            o_g, lse_g = dilated_group(qa[:, :, g], ka[:, :, g], va[:, :, g], slopes_a[g], window, dilation)
            outs.append(o_g)
            lses.append(lse_g)
        alpha = jax.nn.softmax(jnp.stack(lses, axis=0), axis=0)
        o_a = jnp.sum(alpha[..., None] * jnp.stack(outs, axis=0), axis=0)
        o_a = o_a.reshape(Bn, T, A_OUT).astype(x.dtype)

        qb = rmsnorm(proj[..., OFF_QB:OFF_KB].reshape(Bn, T, B_KV_HEADS, B_GROUP, HEAD_DIM), q_norm_b[l])
        kb = rmsnorm(proj[..., OFF_KB:OFF_VB].reshape(Bn, T, B_KV_HEADS, HEAD_DIM), k_norm_b[l])
        vb = proj[..., OFF_VB:OFF_GA].reshape(Bn, T, B_KV_HEADS, HEAD_DIM)
        o_b, _ = banded_attention(qb, kb, vb, slopes_b, 1, B_WINDOW - 1,
                                  sinks_b[l].reshape(B_KV_HEADS, B_GROUP))
        o_b = o_b.reshape(Bn, T, B_Q_WIDTH).astype(x.dtype)

        gate_a = jax.nn.sigmoid(proj[..., OFF_GA:OFF_GB])
        gate_b = jax.nn.sigmoid(proj[..., OFF_GB:IN_COLS])
        merged = gate_a * (o_a @ w_branch_a[l]) + gate_b * (o_b @ w_branch_b[l])
        x = x + merged @ w_out[l]
        h2 = rmsnorm(x, norm2_g[l])
        x = x + jnp.square(jax.nn.relu(h2 @ w_ff1[l])) @ w_ff2[l]
    return x
```

```python
import math
from contextlib import ExitStack

import numpy as np
import ml_dtypes

import concourse.bass as bass
import concourse.mybir as mybir
from concourse.bass_utils import run_bass_kernel_spmd

F32 = mybir.dt.float32
BF16 = mybir.dt.bfloat16
AF = mybir.ActivationFunctionType
ALU = mybir.AluOpType

D = 2048
NCH = 16
TOWN = 2048
TEXT = 4096
DFF = 8192
EPS = 1e-6
NEG = -1.0e9
N_CORES = 8

OFF_QA, OFF_KA, OFF_VA, OFF_QB, OFF_KB, OFF_VB, OFF_GA, OFF_GB = 0, 3072, 6144, 9216, 11264, 11520, 11776, 13824
IN_COLS = 15872


class Grp:
    def __init__(self, name, d, halo, nheads):
        self.name, self.d, self.halo, self.nheads = name, d, halo, nheads
        self.start = TOWN - halo
        self.L = (halo + TOWN) // d
        self.Lq = TOWN // d
        self.nbo = self.Lq // 128
        self.nb = self.nbo + 1
        self.nblk = d * self.nb
        self.ktot = halo + TOWN


GA0 = Grp("a0", 1, 128, 8)
GA1 = Grp("a1", 4, 512, 8)
GA2 = Grp("a2", 16, 2048, 8)
GB = Grp("b", 1, 128, 2)
AGRPS = [GA0, GA1, GA2]


def alibi_slopes():
    i = np.arange(1, 41, dtype=np.float32)
    return (2.0 ** (-8.0 * i / 40)).astype(np.float32)


class Sem:
    def __init__(self, h, name):
        self.h, self.name, self.count = h, name, 0


class Buf:
    def __init__(self, name):
        self.name = name
        self.w = []
        self.r = []
        self.dsem = None


class Prog:
    ENG = ("pe", "act", "dve", "pool", "sp")

    def __init__(self, nc):
        self.nc = nc
        self.es = ExitStack()
        self.q = {e: [] for e in self.ENG}
        self.prog_sem = {}
        self.all_dma_sems = []
        for e in ("pe", "act", "dve", "pool"):
            self.prog_sem[e] = self.sem("prog_" + e)
        self.nsem = 0

    def sem(self, name):
        h = self.es.enter_context(self.nc.semaphore(name))
        return Sem(h, name)

    def _deps(self, reads, writes, eng=None):
        waits = []
        for b in reads:
            waits += b.w
        for b in writes:
            waits += b.w
            waits += b.r
        if eng == "pe":
            me = self.prog_sem["pe"]
            waits = [w for w in waits if w[0] is not me]
        return waits

    def _commit(self, ev, reads, writes):
        for b in reads:
            b.r.append(ev)
        for b in writes:
            b.w = [ev]
            b.r = []

    def op(self, eng, fn, reads=(), writes=(), extra_waits=()):
        waits = self._deps(reads, writes, eng) + list(extra_waits)
        s = self.prog_sem[eng]
        s.count += 1
        ev = (s, s.count)
        self.q[eng].append((waits, fn, s, 1))
        self._commit(ev, reads, writes)
        return ev

    def group(self, eng, fns, reads=(), writes=(), extra_waits=()):
        waits = self._deps(reads, writes, eng) + list(extra_waits)
        s = self.prog_sem[eng]
        s.count += 1
        ev = (s, s.count)
        n = len(fns)
        for i, fn in enumerate(fns):
            self.q[eng].append((waits if i == 0 else [], fn, s if i == n - 1 else None, 1))
        self._commit(ev, reads, writes)
        return ev

    def dma(self, eng, fn, sem_owner, reads=(), writes=(), extra_waits=()):
        if sem_owner.dsem is None:
            sem_owner.dsem = self.sem("d_" + sem_owner.name)
            self.all_dma_sems.append(sem_owner.dsem)
        s = sem_owner.dsem
        waits = self._deps(reads, writes) + list(extra_waits)
        s.count += 16
        ev = (s, s.count)
        self.q[eng].append((waits, fn, s, 16))
        self._commit(ev, reads, writes)
        return ev

    def barrier(self, engines=ENG):
        evs = [(s, s.count) for s in self.prog_sem.values() if s.count > 0]
        evs += [(s, s.count) for s in self.all_dma_sems if s.count > 0]
        for e in engines:
            self.q[e].append((list(evs), None, None, 0))

    def replay(self, eng_name, eng):
        waited = {}
        for waits, fn, s, n in self.q[eng_name]:
            for ws, v in waits:
                if waited.get(ws.name, 0) < v:
                    eng.wait_ge(ws.h, v)
                    waited[ws.name] = v
            if fn is None:
                continue
            ins = fn(eng)
            if s is not None:
                ins.then_inc(s.h, n)


class Arena:
    def __init__(self, nc, limit):
        self.nc, self.off, self.limit, self.n = nc, 16512, limit, 0

    def alloc(self, name, shape, dtype):
        esz = 4 if dtype == F32 else 2
        nbytes = int(np.prod(shape[1:])) * esz
        nbytes = (nbytes + 63) // 64 * 64
        assert self.off + nbytes <= self.limit, f"SBUF overflow allocating {name}: {self.off}+{nbytes}>{self.limit}"
        self.n += 1
        t = self.nc.alloc_sbuf_tensor_at(f"{name}_{self.n}", list(shape), dtype, offset=self.off)
        self.off += nbytes
        return t

    def mark(self):
        return self.off

    def reset(self, m):
        self.off = m


def ap_of(t):
    return t if isinstance(t, bass.AP) else t[:]


def custom_ap(base_ap, extra_off, dims):
    a = ap_of(base_ap)
    return bass.AP(a.tensor, a.offset + extra_off, [list(a.ap[0])] + [list(x) for x in dims])


def build_program(upto=99, debug=False):
    nc = bass.Bass("TRN2", target_bir_lowering=False)
    okind = "ExternalOutput" if debug else "Internal"

    def dram(name, shape, dt, kind):
        return nc.dram_tensor(name, list(shape), dt, kind=kind)

    xe = dram("xe", [TEXT, D], F32, "ExternalInput").ap()
    w_in = dram("w_in", [D, IN_COLS], F32, "ExternalInput").ap()
    norm1_g = dram("norm1_g", [1, D], F32, "ExternalInput").ap()
    norm2_g = dram("norm2_g", [1, D], F32, "ExternalInput").ap()
    gvec_in = dram("gvec", [128, 4], F32, "ExternalInput").ap()
    sinks_in = dram("sinks", [1, 16], F32, "ExternalInput").ap()
    w_bra = dram("w_bra", [1024, D], F32, "ExternalInput").ap()
    w_brb = dram("w_brb", [D, D], F32, "ExternalInput").ap()
    w_out = dram("w_out", [D, D], F32, "ExternalInput").ap()
    w_ff1 = dram("w_ff1", [D, DFF], F32, "ExternalInput").ap()
    w_ff2 = dram("w_ff2", [DFF, D], F32, "ExternalInput").ap()
    ident_in = dram("ident", [128, 128], BF16, "ExternalInput").ap()
    rtab_in = dram("rtab", [128, 6, 512], F32, "ExternalInput").ap()
    out = dram("out", [TOWN, D], F32, "ExternalOutput").ap()

    qs = dram("qs", [40, 128, TOWN], BF16, okind).ap()
    ks = {g.name: dram("ks_" + g.name, [g.nheads, 128, g.ktot], BF16, okind).ap() for g in AGRPS + [GB]}
    vs = {g.name: dram("vs_" + g.name, [g.nheads, 128, g.nblk * 128], BF16, okind).ap() for g in AGRPS + [GB]}
    gs = dram("gs", [32, 128, TOWN], BF16, okind).ap()
    w1c = dram("w1c", [16, 128, 8192], BF16, "Internal").ap()
    w2c = dram("w2c", [8, 128, 16384], BF16, "Internal").ap()
    dbg_ot = dram("dbg_ot", [24, 128, TOWN], BF16, "ExternalOutput").ap() if debug else None
    dbg_mg = dram("dbg_mg", [16, 128, TOWN], BF16, "ExternalOutput").ap() if debug else None
    dbg_h2 = dram("dbg_h2", [16, 128, TOWN], BF16, "ExternalOutput").ap() if debug else None

    P = Prog(nc)
    arena = Arena(nc, 229376)
    psum = [nc.alloc_psum_tensor(f"bank{i}", [128, 512], F32) for i in range(8)]
    bank = [Buf(f"bank{i}") for i in range(8)]

    ident = arena.alloc("ident", [128, 128], BF16)
    ones = arena.alloc("ones", [128, 128], BF16)
    gvec = arena.alloc("gvec", [128, 4], F32)
    gvs = arena.alloc("gvs", [128, 4], F32)
    gv2 = arena.alloc("gv2", [128, 4], F32)
    esink = arena.alloc("esink", [128, 16], F32)
    B_const = Buf("consts")
    wring = [arena.alloc(f"wring{i}", [128, 8192], BF16) for i in range(2)]
    wbuf = [Buf(f"wring{i}") for i in range(2)]
    wctr = [0]
    persist_mark = arena.mark()

    def wslot():
        i = wctr[0] % len(wring)
        wctr[0] += 1
        return wring[i], wbuf[i]

    P.dma("sp", lambda e: e.dma_start(out=ident[:], in_=ident_in[:, :]), B_const, writes=[B_const])
    cb2 = Buf("consts2")
    P.dma("sp", lambda e: e.dma_start(out=gvec[:], in_=gvec_in[:, :]), cb2, writes=[cb2])
    cb3 = Buf("consts3")
    P.dma("sp", lambda e: e.dma_start(out=esink[:], in_=sinks_in.partition_broadcast(128)), cb3, writes=[cb3])
    B_ones = Buf("ones")
    P.op("dve", lambda e: e.memset(ones[:], 1.0), writes=[B_ones])
    B_gvs = Buf("gvs")
    P.op("dve", lambda e: e.tensor_scalar(out=gvs[:], in0=gvec[:], scalar1=math.sqrt(128.0), scalar2=None,
                                          op0=ALU.mult), reads=[cb2], writes=[B_gvs])

    B_gv2 = Buf("gv2")
    P.op("dve", lambda e: e.tensor_scalar(out=gv2[:], in0=gvec[:], scalar1=128.0 ** -0.5, scalar2=None,
                                          op0=ALU.mult), reads=[cb2], writes=[B_gv2])
    pool_hold = []
    bank_rr = [0]

    def next_bank(lo=0, hi=8):
        n = hi - lo
        i = lo + bank_rr[0] % n
        bank_rr[0] += 1
        return i

    m_phase01 = arena.mark()
    hT = arena.alloc("hT", [128, NCH, TEXT], BF16)
    B_hT = Buf("hT")
    m_p0 = arena.mark()

    def norm_transpose_phase(src_rows_ap, ntiles, gain_in, dstT, B_dst, ntok_total, tagp):
        g1b = arena.alloc(tagp + "gb", [128, D], F32)
        B_g1b = Buf(tagp + "gb")
        P.dma("sp", lambda e: e.dma_start(out=g1b[:], in_=gain_in.partition_broadcast(128)), B_g1b, writes=[B_g1b])
        xt = [arena.alloc(f"{tagp}xt{i}", [128, D], F32) for i in range(3)]
        B_xt = [Buf(f"{tagp}xt{i}") for i in range(3)]
        xs = [arena.alloc(f"{tagp}xs{i}", [128, D], BF16) for i in range(2)]
        B_xs = [Buf(f"{tagp}xs{i}") for i in range(2)]
        B_xs2 = [Buf(f"{tagp}xsb{i}") for i in range(2)]
        junk = arena.alloc(tagp + "junk", [128, D], BF16)
        B_junk = Buf(tagp + "junk")
        st = [arena.alloc(f"{tagp}st{i}", [128, 4], F32) for i in range(2)]
        B_st = [Buf(f"{tagp}st{i}") for i in range(2)]
        first_loads = []

        def xload(j):
            x3 = j % 3
            ev = P.dma("sp", lambda e, j=j, x3=x3: e.dma_start(out=xt[x3][:],
                                                               in_=src_rows_ap[128 * j:128 * j + 128, :]),
                       B_xt[x3], writes=[B_xt[x3]])
            if j < 2:
                first_loads.append(ev)

        def stage_a(j):
            s = j % 2
            x3 = j % 3
            if j == 0:
                xload(0)
                xload(1)
            if j + 2 < ntiles:
                xload(j + 2)
            P.op("act", lambda e, s=s, x3=x3: e.activation(out=junk[:], in_=xt[x3][:], func=AF.Square,
                                                          accum_out=st[s][:, 0:1]),
                 reads=[B_xt[x3]], writes=[B_junk, B_st[s]])
            P.op("act", lambda e, s=s: e.activation(out=st[s][:, 1:2], in_=st[s][:, 0:1], func=AF.Ln,
                                                    scale=1.0 / D, bias=EPS), reads=[], writes=[B_st[s]])
            P.op("act", lambda e, s=s: e.activation(out=st[s][:, 2:3], in_=st[s][:, 1:2], func=AF.Exp, scale=-0.5),
                 reads=[], writes=[B_st[s]])
            P.op("dve", lambda e, s=s, x3=x3: e.scalar_tensor_tensor(out=xs[s][:], in0=xt[x3][:],
                                                                      scalar=st[s][:, 2:3],
                                                                      in1=g1b[:], op0=ALU.mult, op1=ALU.mult),
                 reads=[B_xt[x3], B_st[s], B_g1b], writes=[B_xs[s]])

        def stage_b(j):
            s = j % 2
            for qd in range(4):
                bi = next_bank()
                pb = psum[bi][:].bitcast(BF16)
                fns = [(lambda e, c=4 * qd + k, k=k, pb=pb, s=s: e.transpose(out=pb[:, 128 * k:128 * k + 128],
                                                                           in_=xs[s][:, 128 * c:128 * c + 128],
                                                                           identity=ident[:])) for k in range(4)]
                P.group("pe", fns, reads=[B_xs[s], B_const], writes=[bank[bi]])
                dst = dstT[:, 4 * qd:4 * qd + 4, 128 * j:128 * j + 128]
                src = pb[:, 0:512].rearrange("p (a b) -> p a b", a=4)
                if qd == 0:
                    P.op("act", lambda e, dst=dst, src=src: e.copy(out=dst, in_=src), reads=[bank[bi]], writes=[B_dst])
                else:
                    P.op("dve", lambda e, dst=dst, src=src: e.tensor_copy(out=dst, in_=src), reads=[bank[bi]],
                         writes=[B_dst])

        stage_a(0)
        for j in range(ntiles):
            if j + 1 < ntiles:
                stage_a(j + 1)
            stage_b(j)
        return first_loads

    norm_transpose_phase(xe, TEXT // 128, norm1_g, hT, B_hT, TEXT, "p0")
    arena.reset(m_p0)
    P.barrier(("pe", "act", "dve", "sp"))

    if upto >= 1:
        stg = [arena.alloc(f"stg{i}", [128, 2048], BF16) for i in range(3)]
        B_stg = [Buf(f"stg{i}") for i in range(3)]
        stg_ctr = [0]
        vst = [arena.alloc(f"vst{i}", [128, 4, 4, 128], BF16) for i in range(2)]
        B_vst = [[Buf(f"vst{i}_{j}") for j in range(4)] for i in range(2)]
        vst_ctr = [0]
        sq = [arena.alloc(f"sq{i}", [128, 512], BF16) for i in range(3)]
        B_sq = [Buf(f"sq{i}") for i in range(3)]
        lnb = [arena.alloc(f"lnb{i}", [128, 512], F32) for i in range(3)]
        B_lnb = [Buf(f"lnb{i}") for i in range(3)]
        rb = [arena.alloc(f"rb{i}", [128, 512], F32) for i in range(3)]
        B_rb = [Buf(f"rb{i}") for i in range(3)]
        ep_ctr = [0]
        w_in_v = w_in.rearrange("(k p) n -> p k n", p=128)

        def tok_ap(g, c, tau0, n, own_only):
            L = g.Lq if own_only else g.L
            base = TOWN if own_only else g.start
            r0, u0 = divmod(tau0, L)
            if g.d == 1:
                return hT[:, c, base + tau0: base + tau0 + n]
            if u0 + n <= L:
                e0 = base + g.d * u0 + r0
                return custom_ap(hT[:, c, :], e0, [[g.d, n]])
            assert u0 == 0 and n % L == 0
            nr = n // L
            e0 = base + r0
            return custom_ap(hT[:, c, :], e0, [[1, nr], [g.d, L]])

        pending = []

        def flush_pending(keep=0):
            while len(pending) > keep:
                pending.pop(0)()

        def qk_tile(wt, wb, ci, g, own_only, gcol, gscaled, dst_ap):
            total = TOWN if own_only else g.ktot
            L = g.Lq if own_only else g.L
            if g.d == 1:
                blocks = [(t0, min(512, total - t0)) for t0 in range(0, total, 512)]
            elif L >= 512:
                assert L % 512 == 0 or L == 640
                if L == 640:
                    blocks = [(r * L + u0, 320) for r in range(g.d) for u0 in (0, 320)]
                else:
                    blocks = [(r * L + u0, 512) for r in range(g.d) for u0 in range(0, L, 512)]
            else:
                blocks = [(t0, 512) for t0 in range(0, total, 512)]
            gsrc = gvs if gscaled else gvec
            cur = {"slot": None, "c0": 0, "n": 0}

            def flush_stage():
                if cur["slot"] is None or cur["n"] == 0:
                    return
                s, c0, n = cur["slot"], cur["c0"], cur["n"]
                P.dma("sp", lambda e: e.dma_start(out=dst_ap[:, c0:c0 + n], in_=stg[s][:, 0:n]), B_stg[s],
                      reads=[B_stg[s]])
                cur["slot"] = None

            for (t0, n) in blocks:
                if cur["slot"] is None or cur["n"] + n > 2048:
                    pending.append(lambda f=flush_stage_snapshot(cur, dst_ap): f())
                    cur["slot"] = stg_ctr[0] % 3
                    stg_ctr[0] += 1
                    cur["c0"] = t0
                    cur["n"] = 0
                s, off = cur["slot"], cur["n"]
                cur["n"] += n
                bi = next_bank(0, 5)
                fns = []
                for k in range(NCH):
                    fns.append(lambda e, k=k, t0=t0, n=n, bi=bi: e.matmul(
                        psum[bi][:, 0:n], lhsT=wt[:, k * 512 + ci * 128: k * 512 + ci * 128 + 128],
                        rhs=tok_ap(g, k, t0, n, own_only), start=(k == 0), stop=(k == NCH - 1)))
                P.group("pe", fns, reads=[wb, B_hT], writes=[bank[bi]])
                es = ep_ctr[0] % 3
                ep_ctr[0] += 1
                P.op("act", lambda e, bi=bi, n=n, es=es: e.activation(out=sq[es][:, 0:n], in_=psum[bi][:, 0:n],
                                                                    func=AF.Square),
                     reads=[bank[bi]], writes=[B_sq[es]])

                def epilogue(bi=bi, n=n, es=es, s=s, off=off):
                    b2 = 5 + next_bank(0, 3)
                    P.group("pe", [lambda e: e.matmul(psum[b2][:, 0:n], lhsT=ones[:], rhs=sq[es][:, 0:n],
                                                      start=True, stop=True)],
                            reads=[B_sq[es], B_ones], writes=[bank[b2]])
                    P.op("act", lambda e: e.activation(out=lnb[es][:, 0:n], in_=psum[b2][:, 0:n], func=AF.Ln,
                                                       bias=128.0 * EPS), reads=[bank[b2]], writes=[B_lnb[es]])
                    P.op("act", lambda e: e.activation(out=rb[es][:, 0:n], in_=lnb[es][:, 0:n], func=AF.Exp,
                                                       scale=-0.5), reads=[B_lnb[es]], writes=[B_rb[es]])
                    P.op("dve", lambda e: e.scalar_tensor_tensor(out=stg[s][:, off:off + n], in0=psum[bi][:, 0:n],
                                                                 scalar=gsrc[:, gcol:gcol + 1], in1=rb[es][:, 0:n],
                                                                 op0=ALU.mult, op1=ALU.mult),
                         reads=[bank[bi], B_rb[es], B_gvs, cb2], writes=[B_stg[s]])

                pending.append(epilogue)
                flush_pending(keep=2)
            pending.append(lambda f=flush_stage_snapshot(cur, dst_ap): f())

        def flush_stage_snapshot(cur, dst_ap):
            s, c0, n = cur["slot"], cur["c0"], cur["n"]

            def f():
                if s is None or n == 0:
                    return
                P.dma("sp", lambda e: e.dma_start(out=dst_ap[:, c0:c0 + n], in_=stg[s][:, 0:n]), B_stg[s],
                      reads=[B_stg[s]])
            return f

        def v_slab(wt, wb, col0, ncols, g, h0):
            nh = ncols // 128
            vsd = vs[g.name]
            bb = 0
            slot = None
            b0 = 0
            for b in range(g.nblk):
                r, jb = divmod(b, g.nb)
                if slot is None:
                    slot = vst_ctr[0] % 2
                    vst_ctr[0] += 1
                    bb = 0
                    b0 = b
                e0 = g.start + g.d * 128 * jb + r
                bi = next_bank(0, 5)
                fns = []
                for k in range(NCH):
                    lhs = hT[:, k, e0:e0 + 128] if g.d == 1 else custom_ap(hT[:, k, :], e0, [[g.d, 128]])
                    fns.append(lambda e, k=k, lhs=lhs, bi=bi: e.matmul(
                        psum[bi][:, 0:ncols], lhsT=lhs, rhs=wt[:, k * 512 + col0: k * 512 + col0 + ncols],
                        start=(k == 0), stop=(k == NCH - 1)))
                P.group("pe", fns, reads=[wb, B_hT], writes=[bank[bi]])
                dst = vst[slot][:, 0:nh, bb, :]
                src = psum[bi][:, 0:ncols].rearrange("p (a b) -> p a b", a=nh)
                if b % 2 == 0:
                    P.op("act", lambda e, dst=dst, src=src: e.copy(out=dst, in_=src), reads=[bank[bi]],
                         writes=[B_vst[slot][bb]])
                else:
                    P.op("dve", lambda e, dst=dst, src=src: e.tensor_copy(out=dst, in_=src), reads=[bank[bi]],
                         writes=[B_vst[slot][bb]])
                bb += 1
                if bb == 4 or b == g.nblk - 1:
                    for hh in range(nh):
                        P.dma("sp", lambda e, slot=slot, hh=hh, b0=b0, bb=bb: e.dma_start(
                            out=vsd[h0 + hh, :, 128 * b0:128 * (b0 + bb)],
                            in_=vst[slot][:, hh, 0:bb, :].rearrange("p a b -> p (a b)")),
                            B_vst[slot][hh], reads=B_vst[slot][0:bb])
                    slot = None

        def gate_tile(wt, wb, ci, gidx):
            s = stg_ctr[0] % 3
            stg_ctr[0] += 1
            for tb in range(4):
                bi = next_bank(0, 5)
                fns = []
                for k in range(NCH):
                    fns.append(lambda e, k=k, tb=tb, bi=bi: e.matmul(
                        psum[bi][:, :], lhsT=wt[:, k * 512 + ci * 128: k * 512 + ci * 128 + 128],
                        rhs=hT[:, k, TOWN + 512 * tb: TOWN + 512 * tb + 512], start=(k == 0), stop=(k == NCH - 1)))
                P.group("pe", fns, reads=[wb, B_hT], writes=[bank[bi]])
                P.op("act", lambda e, tb=tb, bi=bi, s=s: e.activation(out=stg[s][:, 512 * tb:512 * tb + 512],
                                                                    in_=psum[bi][:, :], func=AF.Sigmoid),
                     reads=[bank[bi]], writes=[B_stg[s]])
            P.dma("sp", lambda e, s=s: e.dma_start(out=gs[gidx, :, :], in_=stg[s][:, :]), B_stg[s], reads=[B_stg[s]])


        xn = [arena.alloc(f"xn{i}", [128, 4, 128], BF16) for i in range(3)]
        B_xn = [Buf(f"xn{i}") for i in range(3)]
        jk = arena.alloc("jk", [128, 128], BF16)
        B_jk = Buf("jk")
        sst = [arena.alloc(f"sst{i}", [128, 12], F32) for i in range(3)]
        B_sst = [Buf(f"sst{i}") for i in range(3)]
        tm_ctr = [0]

        def qk_slab_tm(wt, wb, g, h0, is_k, dst_heads):
            nblk = g.nblk if is_k else g.d * g.nbo
            per = g.nb if is_k else g.nbo
            base = g.start if is_k else TOWN
            gsc = gv2[:, 1:2] if is_k else gvec[:, 0:1]
            state = {"slot": None, "bb": 0, "b0": 0}
            for b in range(nblk):
                r, jb = divmod(b, per)
                e0 = base + g.d * 128 * jb + r
                bi = next_bank(0, 5)
                fns = []
                for k in range(NCH):
                    lhs = custom_ap(hT[:, k, :], e0, [[g.d, 128]])
                    fns.append(lambda e, k=k, lhs=lhs, bi=bi: e.matmul(
                        psum[bi][:, :], lhsT=lhs, rhs=wt[:, k * 512: k * 512 + 512],
                        start=(k == 0), stop=(k == NCH - 1)))
                P.group("pe", fns, reads=[wb, B_hT], writes=[bank[bi]])
                ts_ = tm_ctr[0] % 3
                tm_ctr[0] += 1
                for hh in range(4):
                    P.op("act", lambda e, bi=bi, hh=hh, ts_=ts_: e.activation(
                        out=jk[:], in_=psum[bi][:, 128 * hh:128 * hh + 128], func=AF.Square,
                        accum_out=sst[ts_][:, hh:hh + 1]), reads=[bank[bi]], writes=[B_jk, B_sst[ts_]])
                P.op("act", lambda e, ts_=ts_: e.activation(out=sst[ts_][:, 4:8], in_=sst[ts_][:, 0:4], func=AF.Ln,
                                                            scale=1.0 / 128, bias=EPS), reads=[], writes=[B_sst[ts_]])
                P.op("act", lambda e, ts_=ts_: e.activation(out=sst[ts_][:, 8:12], in_=sst[ts_][:, 4:8], func=AF.Exp,
                                                            scale=-0.5), reads=[], writes=[B_sst[ts_]])
                P.op("dve", lambda e, bi=bi, ts_=ts_: e.tensor_tensor(
                    out=xn[ts_][:], in0=psum[bi][:, :].rearrange("p (a b) -> p a b", a=4),
                    in1=custom_ap(sst[ts_], 8, [[1, 4], [0, 128]]), op=ALU.mult),
                    reads=[bank[bi], B_sst[ts_]], writes=[B_xn[ts_]])
                if state["slot"] is None:
                    state["slot"] = vst_ctr[0] % 2
                    vst_ctr[0] += 1
                    state["bb"] = 0
                    state["b0"] = b
                slot, bb, b0 = state["slot"], state["bb"], state["b0"]
                state["bb"] += 1
                last = (state["bb"] == 4 or b == nblk - 1)
                if last:
                    state["slot"] = None

                def epilogue(ts_=ts_, slot=slot, bb=bb, b0=b0, last=last, b=b):
                    b2 = 5 + next_bank(0, 3)
                    pb = psum[b2][:].bitcast(BF16)
                    fns = [(lambda e, hh=hh: e.transpose(out=pb[:, 128 * hh:128 * hh + 128], in_=xn[ts_][:, hh, :],
                                                         identity=ident[:])) for hh in range(4)]
                    P.group("pe", fns, reads=[B_xn[ts_], B_const], writes=[bank[b2]])
                    dst = vst[slot][:, 0:4, bb, :]
                    src = pb[:, 0:512].rearrange("p (a b) -> p a b", a=4)
                    if b % 2 == 0:
                        P.op("act", lambda e: e.activation(out=dst, in_=src, func=AF.Copy, scale=gsc),
                             reads=[bank[b2], cb2, B_gv2], writes=[B_vst[slot][bb]])
                    else:
                        P.op("dve", lambda e: e.tensor_scalar(out=dst, in0=src, scalar1=gsc, scalar2=None,
                                                              op0=ALU.mult),
                             reads=[bank[b2], cb2, B_gv2], writes=[B_vst[slot][bb]])
                    if last:
                        nbb = bb + 1
                        for hh in range(4):
                            P.dma("sp", lambda e, hh=hh: e.dma_start(
                                out=dst_heads[h0 + hh][:, 128 * b0:128 * (b0 + nbb)],
                                in_=vst[slot][:, hh, 0:nbb, :].rearrange("p a b -> p (a b)")),
                                B_vst[slot][hh], reads=B_vst[slot][0:nbb])

                pending.append(epilogue)
                flush_pending(keep=2)
            flush_pending()

        nslab = IN_COLS // 512
        B_cache = Buf("wcache")
        ff1_vc = w_ff1.rearrange("(k p) n -> p k n", p=128)
        ff2_vc = w_ff2.rearrange("(j p) m -> p j m", p=128)
        for sl in range(nslab):
            wt, wb = wslot()
            P.dma("pool", lambda e, sl=sl, wt=wt: e.dma_start(
                out=wt[:].rearrange("p (k n) -> p k n", k=NCH), in_=w_in_v[:, :, 512 * sl:512 * sl + 512]),
                wb, writes=[wb])
            if sl < 16:
                P.dma("pool", lambda e, sl=sl: e.dma_start(
                    out=w1c[sl].rearrange("p (k n) -> p k n", k=NCH), in_=ff1_vc[:, :, 512 * sl:512 * sl + 512]),
                    B_cache, writes=[B_cache])
            elif sl < 24:
                ci_ = sl - 16
                P.dma("pool", lambda e, ci_=ci_: e.dma_start(
                    out=w2c[ci_].rearrange("p (j m) -> p j m", j=32),
                    in_=ff2_vc[:, 32 * (ci_ % 2):32 * (ci_ % 2) + 32, 512 * (ci_ // 2):512 * (ci_ // 2) + 512]),
                    B_cache, writes=[B_cache])
            for ci in range(4):
                cc = 4 * sl + ci
                col = 128 * cc
                if col < OFF_KA:
                    gi, h = divmod(cc, 8)
                    if gi == 0:
                        qk_tile(wt, wb, ci, AGRPS[gi], True, 0, True, qs[gi * 8 + h])
                    elif ci == 0:
                        flush_pending()
                        qk_slab_tm(wt, wb, AGRPS[gi], h, False, [qs[gi * 8 + x] for x in range(8)])
                elif col < OFF_VA:
                    gi, h = divmod(cc - 24, 8)
                    if gi == 0:
                        qk_tile(wt, wb, ci, AGRPS[gi], False, 1, False, ks[AGRPS[gi].name][h])
                    elif ci == 0:
                        flush_pending()
                        qk_slab_tm(wt, wb, AGRPS[gi], h, True, [ks[AGRPS[gi].name][x] for x in range(8)])
                elif col < OFF_QB:
                    if ci == 0:
                        flush_pending()
                        gi, hh = divmod(sl - 12, 2)
                        v_slab(wt, wb, 0, 512, AGRPS[gi], 4 * hh)
                elif col < OFF_KB:
                    j = cc - 72
                    qk_tile(wt, wb, ci, GB, True, 2, True, qs[24 + j])
                elif col < OFF_VB:
                    qk_tile(wt, wb, ci, GB, False, 3, False, ks["b"][cc - 88])
                elif col < OFF_GA:
                    if ci == 2:
                        flush_pending()
                        v_slab(wt, wb, 256, 256, GB, 0)
                else:
                    flush_pending()
                    gate_tile(wt, wb, ci, cc - 92)
        flush_pending()

    arena.reset(m_phase01)
    P.barrier()


    slopes = alibi_slopes()
    if upto >= 2:
        oT = arena.alloc("oT", [128, 24, TOWN], BF16)
        B_oT = Buf("oT")
        m_p2 = arena.mark()
        rtab = arena.alloc("rtab", [128, 6, 512], F32)
        B_rtab = Buf("rtab")
        P.dma("sp", lambda e: e.dma_start(out=rtab[:], in_=rtab_in[:, :, :]), B_rtab, writes=[B_rtab])
        Vt = [arena.alloc(f"Vt{i}", [128, 32, 128], BF16) for i in range(2)]
        B_Vt = [Buf(f"Vt{i}") for i in range(2)]
        Qt = [wring[i][:, 0:2048] for i in range(2)]
        Kt = [wring[i][:, 2048:6144] for i in range(2)]
        PT = [wring[0][:, 6144:7168], wring[1][:, 6144:7168], wring[0][:, 7168:8192]]
        B_Qt = [Buf(f"Qt{i}") for i in range(2)]
        B_Kt = [Buf(f"Kt{i}") for i in range(2)]
        B_PTa = [Buf(f"PTa{i}") for i in range(3)]
        B_PTb = [Buf(f"PTb{i}") for i in range(3)]
        Oacc = arena.alloc("Oacc", [128, TOWN], F32)
        Dacc = arena.alloc("Dacc", [128, TOWN], F32)
        B_Oc = [Buf(f"Oacc{c}") for c in range(4)]
        B_Dc = [Buf(f"Dacc{c}") for c in range(4)]
        Et = [[arena.alloc(f"E{i}_{j}", [128, 512], F32) for j in range(3)] for i in range(2)]
        B_Et = [[Buf(f"E{i}_{j}") for j in range(3)] for i in range(2)]
        Pexp = [arena.alloc(f"Pexp{i}", [128, 1024], F32) for i in range(3)]
        B_Pexp = [[Buf(f"Pexp{i}_{j}") for j in range(2)] for i in range(3)]
        rec = [arena.alloc(f"rec{i}", [128, 512], F32) for i in range(2)]
        B_rec = [Buf(f"rec{i}") for i in range(2)]
        esx = arena.alloc("esx", [128, 16], F32)
        B_esx = Buf("esx")
        P.op("act", lambda e: e.activation(out=esx[:], in_=esink[:], func=AF.Exp), reads=[cb3], writes=[B_esx])

        ctr = {"unit": 0, "prep": 0, "fill": 0, "s": 0, "o": 0, "d": 0, "rec": 0}
        pend2 = []

        def flush2():
            while pend2:
                pend2.pop(0)()

        fin = {"slot": None}

        def finalize_chunk(h, c):
            cs = slice(512 * c, 512 * c + 512)
            P.op("act", lambda e, cs=cs: e.activation(out=Dacc[:, cs], in_=Dacc[:, cs], func=AF.Ln), reads=[],
                 writes=[B_Dc[c]])
            P.op("act", lambda e, cs=cs: e.activation(out=Dacc[:, cs], in_=Dacc[:, cs], func=AF.Exp, scale=-1.0),
                 reads=[], writes=[B_Dc[c]])
            P.op("dve", lambda e, h=h, cs=cs: e.tensor_tensor(out=oT[:, h, cs], in0=Oacc[:, cs], in1=Dacc[:, cs],
                                                              op=ALU.mult),
                 reads=[B_Oc[c], B_Dc[c]], writes=[B_oT])

        def prep_unit(g, kvh, qi, a, is_b, slot, first_group, bj):
            u = ctr["prep"] % 2
            ctr["prep"] += 1
            P.dma("sp", lambda e: e.dma_start(out=Qt[u], in_=qs[qi]), B_Qt[u], writes=[B_Qt[u]])
            P.dma("sp", lambda e: e.dma_start(out=Kt[u][:, 0:g.ktot], in_=ks[g.name][kvh]), B_Kt[u], writes=[B_Kt[u]])
            P.dma("sp", lambda e: e.dma_start(out=Vt[u][:, 0:g.nblk, :].rearrange("p a b -> p (a b)"),
                                              in_=vs[g.name][kvh]), B_Vt[u], writes=[B_Vt[u]])
            hidx = 5 if is_b else (4 if g.d == 16 else 3)
            pidx = 2 if is_b else 1
            P.op("act", lambda e: e.activation(out=Et[u][0][:], in_=rtab[:, 0, :], func=AF.Exp, scale=float(a)),
                 reads=[B_rtab], writes=[B_Et[u][0]])
            P.op("act", lambda e: e.activation(out=Et[u][1][:], in_=rtab[:, hidx, :], func=AF.Exp, scale=float(a)),
                 reads=[B_rtab], writes=[B_Et[u][1]])
            if g.d == 1:
                P.op("act", lambda e: e.activation(out=Et[u][2][:], in_=rtab[:, pidx, :], func=AF.Exp,
                                                   scale=float(a)), reads=[B_rtab], writes=[B_Et[u][2]])

        def attn_unit(g, kvh, qi, a, is_b, slot, first_group, bj, next_unit=None):
            u = ctr["unit"] % 2
            ctr["unit"] += 1
            def od_part(f, fp, ent):
                ob = 4 + ctr["o"] % 2
                ctr["o"] += 1
                db = 6 + ctr["d"] % 2
                ctr["d"] += 1
                fns = []
                for j, (qc, bp) in enumerate(ent):
                    fns.append(lambda e, j=j, bp=bp, ob=ob, fp=fp: e.matmul(
                        psum[ob][:, 128 * j:128 * j + 128], lhsT=Vt[u][:, bp, :], rhs=PT[fp][:, 128 * j:128 * j + 128],
                        start=True, stop=False))
                    fns.append(lambda e, j=j, bp=bp, ob=ob, fp=fp: e.matmul(
                        psum[ob][:, 128 * j:128 * j + 128], lhsT=Vt[u][:, bp + 1, :],
                        rhs=PT[fp][:, 512 + 128 * j:512 + 128 * j + 128], start=False, stop=True))
                P.group("pe", fns, reads=[B_Vt[u], B_PTa[fp], B_PTb[fp]], writes=[bank[ob]])
                fns = [lambda e, db=db, fp=fp: e.matmul(psum[db][:, :], lhsT=ones[:], rhs=PT[fp][:, 0:512],
                                                        start=True, stop=False),
                       lambda e, db=db, fp=fp: e.matmul(psum[db][:, :], lhsT=ones[:], rhs=PT[fp][:, 512:1024],
                                                        start=False, stop=True)]
                P.group("pe", fns, reads=[B_ones, B_PTa[fp], B_PTb[fp]], writes=[bank[db]])
                if is_b:
                    rs_ = ctr["rec"] % 2
                    ctr["rec"] += 1
                    P.op("act", lambda e, db=db, rs_=rs_: e.activation(
                        out=rec[rs_][:], in_=psum[db][:, :], func=AF.Ln, bias=esx[:, bj:bj + 1]),
                        reads=[bank[db], B_esx], writes=[B_rec[rs_]])
                    P.op("act", lambda e, rs_=rs_: e.activation(out=rec[rs_][:], in_=rec[rs_][:], func=AF.Exp,
                                                                scale=-1.0), reads=[], writes=[B_rec[rs_]])
                    P.op("dve", lambda e, ob=ob, rs_=rs_, f=f: e.tensor_tensor(
                        out=oT[:, slot, 512 * f:512 * f + 512], in0=psum[ob][:, :], in1=rec[rs_][:], op=ALU.mult),
                        reads=[bank[ob], B_rec[rs_]], writes=[B_oT])
                else:
                    if g.d == 1:
                        ov, dv, po, pd = (Oacc[:, 512 * f:512 * f + 512], Dacc[:, 512 * f:512 * f + 512],
                                          psum[ob][:, :], psum[db][:, :])
                    elif g.d == 4:
                        ov, dv, po, pd = Oacc[:, f:TOWN:4], Dacc[:, f:TOWN:4], psum[ob][:, :], psum[db][:, :]
                    else:
                        ov = custom_ap(Oacc, 4 * f, [[1, 4], [16, 128]])
                        dv = custom_ap(Dacc, 4 * f, [[1, 4], [16, 128]])
                        po = psum[ob][:, :].rearrange("p (j i) -> p j i", j=4)
                        pd = psum[db][:, :].rearrange("p (j i) -> p j i", j=4)
                    if first_group:
                        if fin["slot"] is not None:
                            finalize_chunk(fin["slot"], f)
                            if f == 3:
                                fin["slot"] = None
                        P.op("act", lambda e, ov=ov, po=po: e.copy(out=ov, in_=po), reads=[bank[ob]], writes=[B_Oc[f]])
                        P.op("dve", lambda e, dv=dv, pd=pd: e.tensor_copy(out=dv, in_=pd), reads=[bank[db]],
                             writes=[B_Dc[f]])
                    else:
                        P.op("dve", lambda e, ov=ov, po=po: e.tensor_tensor(out=ov, in0=po, in1=ov, op=ALU.add),
                             reads=[bank[ob]], writes=B_Oc)
                        P.op("dve", lambda e, dv=dv, pd=pd: e.tensor_tensor(out=dv, in0=pd, in1=dv, op=ALU.add),
                             reads=[bank[db]], writes=B_Dc)

            for f in range(4):
                if g.d == 1:
                    ent = [(128 * (4 * f + j), 4 * f + j) for j in range(4)]
                    eprev = 1 if f == 0 else 2
                elif g.d == 4:
                    ent = [(128 * (4 * f + j), 5 * f + j) for j in range(4)]
                    eprev = 1
                else:
                    ent = [(128 * (4 * f + j), 2 * (4 * f + j)) for j in range(4)]
                    eprev = 1
                fp = ctr["fill"] % 3
                ctr["fill"] += 1
                sp_, sc_ = ctr["s"] % 4, (ctr["s"] + 1) % 4
                ctr["s"] += 2
                for sb_, role in ((sp_, 0), (sc_, 1)):
                    fns = [(lambda e, j=j, qc=qc, bp=bp, sb_=sb_, role=role: e.matmul(
                        psum[sb_][:, 128 * j:128 * j + 128], lhsT=Kt[u][:, 128 * (bp + role):128 * (bp + role) + 128],
                        rhs=Qt[u][:, qc:qc + 128], start=True, stop=True)) for j, (qc, bp) in enumerate(ent)]
                    P.group("pe", fns, reads=[B_Qt[u], B_Kt[u]], writes=[bank[sb_]])
                    P.op("act", lambda e, sb_=sb_, role=role, fp=fp: e.activation(
                        out=Pexp[fp][:, 512 * role:512 * role + 512], in_=psum[sb_][:, :], func=AF.Exp),
                        reads=[bank[sb_]], writes=[B_Pexp[fp][role]])
                P.op("dve" if (is_b or first_group) else "pool", lambda e, fp=fp, eprev=eprev: e.tensor_tensor(
                    out=PT[fp][:, 0:512], in0=Pexp[fp][:, 0:512], in1=Et[u][eprev][:], op=ALU.mult),
                    reads=[B_Pexp[fp][0], B_Et[u][eprev]], writes=[B_PTa[fp]])
                P.op("pool", lambda e, fp=fp: e.tensor_tensor(
                    out=PT[fp][:, 512:1024], in0=Pexp[fp][:, 512:1024], in1=Et[u][0][:], op=ALU.mult),
                    reads=[B_Pexp[fp][1], B_Et[u][0]], writes=[B_PTb[fp]])
                pend2.append(lambda f=f, fp=fp, ent=ent: od_part(f, fp, ent))
                while len(pend2) > 2:
                    pend2.pop(0)()
                if f == 1 and next_unit is not None:
                    prep_unit(*next_unit)

        units = []
        for h in range(8):
            for gi, g in enumerate(AGRPS):
                units.append((g, h, gi * 8 + h, slopes[16 + 8 * gi + h] * g.d, False, h, gi == 0, 0))
        for j in range(16):
            units.append((GB, j // 8, 24 + j, slopes[j], True, 8 + j, False, j))
        prep_unit(*units[0])
        for i, un in enumerate(units):
            attn_unit(*un, next_unit=(units[i + 1] if i + 1 < len(units) else None))
            if (not un[4]) and un[0] is GA2:
                if un[1] < 7:
                    fin["slot"] = un[1]
                else:
                    flush2()
                    for c in range(4):
                        finalize_chunk(7, c)
        flush2()
        if debug:
            P.dma("sp", lambda e: e.dma_start(out=dbg_ot.rearrange("c p t -> p c t"), in_=oT[:]), B_oT, reads=[B_oT])
        arena.reset(m_p2)
        P.barrier()

    if upto >= 3:
        mT = arena.alloc("mT", [128, NCH, TOWN], BF16)
        B_mT = Buf("mT")
        m_p3 = arena.mark()
        gat = [[arena.alloc(f"gat{i}_{j}", [128, 512], BF16) for j in range(2)] for i in range(3)]
        B_gat = [[Buf(f"gat{i}_{j}") for j in range(2)] for i in range(3)]
        tmp = [arena.alloc(f"tmp{j}", [128, 512], F32) for j in range(2)]
        B_tmp = [Buf(f"tmp{j}") for j in range(2)]
        B_w2 = [Buf("wring0b"), Buf("wring1b")]
        bra_v = w_bra.rearrange("(k p) n -> p k n", p=128)
        brb_v = w_brb.rearrange("(k p) n -> p k n", p=128)
        gctr = 0
        for n in range(NCH):
            wi = wctr[0] % 2
            wt, wb = wslot()
            wb2 = B_w2[wi]
            P.dma("pool", lambda e, wt=wt, n=n: e.dma_start(
                out=wt[:, 0:1024].rearrange("p (k n) -> p k n", k=8), in_=bra_v[:, :, 128 * n:128 * n + 128]),
                wb, writes=[wb])
            P.dma("pool", lambda e, wt=wt, n=n: e.dma_start(
                out=wt[:, 1024:3072].rearrange("p (k n) -> p k n", k=16), in_=brb_v[:, :, 128 * n:128 * n + 128]),
                wb2, writes=[wb2])
            for tb in range(4):
                gsl = gctr % 3
                gctr += 1
                for j in range(2):
                    P.dma("sp", lambda e, gsl=gsl, j=j, n=n, tb=tb: e.dma_start(
                        out=gat[gsl][j][:], in_=gs[16 * j + n, :, 512 * tb:512 * tb + 512]),
                        B_gat[gsl][j], writes=[B_gat[gsl][j]])
                b1 = next_bank()
                b2 = next_bank()
                fns = [(lambda e, k=k, tb=tb, b1=b1, wt=wt: e.matmul(
                    psum[b1][:, :], lhsT=wt[:, 128 * k:128 * k + 128], rhs=oT[:, k, 512 * tb:512 * tb + 512],
                    start=(k == 0), stop=(k == 7))) for k in range(8)]
                P.group("pe", fns, reads=[wb, B_oT], writes=[bank[b1]])
                fns = [(lambda e, k=k, tb=tb, b2=b2, wt=wt: e.matmul(
                    psum[b2][:, :], lhsT=wt[:, 1024 + 128 * k:1024 + 128 * k + 128],
                    rhs=oT[:, 8 + k, 512 * tb:512 * tb + 512], start=(k == 0), stop=(k == 15))) for k in range(16)]
                P.group("pe", fns, reads=[wb2, wb, B_oT], writes=[bank[b2]])
                P.op("dve", lambda e, b1=b1, gsl=gsl: e.tensor_tensor(out=tmp[0][:], in0=psum[b1][:, :],
                                                                     in1=gat[gsl][0][:], op=ALU.mult),
                     reads=[bank[b1], B_gat[gsl][0]], writes=[B_tmp[0]])
                P.op("dve", lambda e, b2=b2, gsl=gsl: e.tensor_tensor(out=tmp[1][:], in0=psum[b2][:, :],
                                                                     in1=gat[gsl][1][:], op=ALU.mult),
                     reads=[bank[b2], B_gat[gsl][1]], writes=[B_tmp[1]])
                P.op("dve", lambda e, n=n, tb=tb: e.tensor_tensor(out=mT[:, n, 512 * tb:512 * tb + 512],
                                                                  in0=tmp[0][:], in1=tmp[1][:], op=ALU.add),
                     reads=[B_tmp[0], B_tmp[1]], writes=[B_mT])
        if debug:
            P.dma("sp", lambda e: e.dma_start(out=dbg_mg.rearrange("c p t -> p c t"), in_=mT[:]), B_mT, reads=[B_mT])
        arena.reset(m_p3)
        P.barrier(("pe", "act", "dve", "sp"))

    if upto >= 4:
        NXO = 6
        xo = [arena.alloc(f"xo{i}", [128, 512], F32) for i in range(NXO)]
        B_xo = [Buf(f"xo{i}") for i in range(NXO)]
        xctr = 0
        wout_v = w_out.rearrange("(k p) n -> p k n", p=128)
        its = [(ms, tt) for ms in range(4) for tt in range(16)]

        def xload(i):
            ms, tt = its[i]
            P.dma("sp", lambda e, i=i, tt=tt, ms=ms: e.dma_start(
                out=xo[i % NXO][:], in_=xe[TOWN + 128 * tt:TOWN + 128 * tt + 128, 512 * ms:512 * ms + 512]),
                B_xo[i % NXO], writes=[B_xo[i % NXO]])

        XPF = 3
        for i in range(XPF):
            xload(i)
        wt = wb = None
        for i, (ms, tt) in enumerate(its):
            if tt == 0:
                wt, wb = wslot()
                P.dma("pool", lambda e, wt=wt, ms=ms: e.dma_start(
                    out=wt[:].rearrange("p (k n) -> p k n", k=NCH), in_=wout_v[:, :, 512 * ms:512 * ms + 512]),
                    wb, writes=[wb])
            if i + XPF < len(its):
                xload(i + XPF)
            xs_ = i % NXO
            bi = next_bank()
            fns = [(lambda e, k=k, tt=tt, bi=bi, wt=wt: e.matmul(
                psum[bi][:, :], lhsT=mT[:, k, 128 * tt:128 * tt + 128], rhs=wt[:, 512 * k:512 * k + 512],
                start=(k == 0), stop=(k == NCH - 1))) for k in range(NCH)]
            P.group("pe", fns, reads=[wb, B_mT], writes=[bank[bi]])
            P.op("dve", lambda e, xs_=xs_, bi=bi: e.tensor_tensor(out=xo[xs_][:], in0=psum[bi][:, :],
                                                                 in1=xo[xs_][:], op=ALU.add),
                 reads=[bank[bi]], writes=[B_xo[xs_]])
            P.dma("sp", lambda e, xs_=xs_, tt=tt, ms=ms: e.dma_start(
                out=out[128 * tt:128 * tt + 128, 512 * ms:512 * ms + 512], in_=xo[xs_][:]),
                B_xo[xs_], reads=[B_xo[xs_]])
        P.barrier()
        arena.reset(persist_mark)

    if upto >= 5:
        h2T = arena.alloc("h2T", [128, NCH, TOWN], BF16)
        B_h2T = Buf("h2T")
        m_p4 = arena.mark()
        hold = norm_transpose_phase(out, TOWN // 128, norm2_g, h2T, B_h2T, TOWN, "p4")
        pool_hold.extend(hold)
        if debug:
            P.dma("sp", lambda e: e.dma_start(out=dbg_h2.rearrange("c p t -> p c t"), in_=h2T[:]), B_h2T,
                  reads=[B_h2T])
        P.barrier(("pe", "act", "dve", "sp"))
        late_barrier = [(s_, s_.count) for s_ in P.prog_sem.values() if s_.count > 0]
        late_barrier += [(s_, s_.count) for s_ in P.all_dma_sems if s_.count > 0]
        arena.reset(m_p4)
    if upto >= 6:
        uT = arena.alloc("uT", [128, 64, 512], BF16)
        B_uT = Buf("uT")
        rt = [arena.alloc(f"rt{i}", [128, 512], F32) for i in range(2)]
        B_rt = [Buf(f"rt{i}") for i in range(2)]
        yo = [arena.alloc(f"yo{i}", [128, 512], F32) for i in range(4)]
        B_yo = [Buf(f"yo{i}") for i in range(4)]
        for i in range(2):
            wring.append(arena.alloc(f"wringx{i}", [128, 8192], BF16))
            wbuf.append(Buf(f"wringx{i}"))
        xctr = 0
        rctr = 0
        ff1_v = w_ff1.rearrange("(k p) n -> p k n", p=128)
        ff2_v = w_ff2.rearrange("(j p) m -> p j m", p=128)
        wctr[0] = 0
        nslab5 = [0]
        for tb in range(4):
            for sl in range(16):
                wt, wb = wslot()
                xw = []
                if nslab5[0] == 0:
                    xw = list(pool_hold)
                elif nslab5[0] == 2:
                    xw = list(late_barrier)
                nslab5[0] += 1
                P.dma("pool", lambda e, wt=wt, sl=sl: e.dma_start(out=wt[:], in_=w1c[sl]),
                      wb, writes=[wb], extra_waits=xw)
                for ci in range(4):
                    bi = next_bank(0, 8)
                    fns = [(lambda e, k=k, ci=ci, bi=bi, wt=wt, tb=tb: e.matmul(
                        psum[bi][:, :], lhsT=wt[:, 512 * k + 128 * ci:512 * k + 128 * ci + 128],
                        rhs=h2T[:, k, 512 * tb:512 * tb + 512], start=(k == 0), stop=(k == NCH - 1)))
                        for k in range(NCH)]
                    P.group("pe", fns, reads=[wb, B_h2T], writes=[bank[bi]])
                    rs_ = rctr % 2
                    rctr += 1
                    P.op("act", lambda e, bi=bi, rs_=rs_: e.activation(out=rt[rs_][:], in_=psum[bi][:, :],
                                                                      func=AF.Relu),
                         reads=[bank[bi]], writes=[B_rt[rs_]])
                    P.op("dve", lambda e, bi=bi, rs_=rs_, sl=sl, ci=ci: e.scalar_tensor_tensor(
                        out=uT[:, 4 * sl + ci, :], in0=psum[bi][:, :], scalar=0.0, in1=rt[rs_][:],
                        op0=ALU.max, op1=ALU.mult), reads=[bank[bi], B_rt[rs_]], writes=[B_uT])
            for ms in range(4):
                pb0 = 4 if (4 * tb + ms) % 2 == 0 else 0
                for tt in range(4):
                    r0 = 512 * tb + 128 * tt
                    P.dma("sp", lambda e, tt=tt, r0=r0, ms=ms: e.dma_start(
                        out=yo[tt][:], in_=out[r0:r0 + 128, 512 * ms:512 * ms + 512]), B_yo[tt],
                        writes=[B_yo[tt]])
                for jg in range(8):
                    wt, wb = wslot()
                    P.dma("pool", lambda e, wt=wt, jg=jg, ms=ms: e.dma_start(
                        out=wt[:, 0:4096], in_=w2c[2 * ms + jg // 4][:, 4096 * (jg % 4):4096 * (jg % 4) + 4096]),
                        wb, writes=[wb])
                    fns = []
                    for jj in range(8):
                        for tt in range(4):
                            fns.append(lambda e, jj=jj, tt=tt, jg=jg, wt=wt, pb0=pb0: e.matmul(
                                psum[pb0 + tt][:, :], lhsT=uT[:, 8 * jg + jj, 128 * tt:128 * tt + 128],
                                rhs=wt[:, 512 * jj:512 * jj + 512], start=(jg == 0 and jj == 0),
                                stop=(jg == 7 and jj == 7)))
                    P.group("pe", fns, reads=[wb, B_uT], writes=[bank[pb0 + t_] for t_ in range(4)])
                for tt in range(4):
                    r0 = 512 * tb + 128 * tt
                    P.op("dve", lambda e, tt=tt, pb0=pb0: e.tensor_tensor(out=yo[tt][:], in0=psum[pb0 + tt][:, :],
                                                                         in1=yo[tt][:], op=ALU.add),
                         reads=[bank[pb0 + tt]], writes=[B_yo[tt]])
                    P.dma("sp", lambda e, tt=tt, r0=r0, ms=ms: e.dma_start(
                        out=out[r0:r0 + 128, 512 * ms:512 * ms + 512], in_=yo[tt][:]), B_yo[tt],
                        reads=[B_yo[tt]])

    P.barrier(("sp",))

    with nc.Block() as block:
        @block.sync
        def _(e):
            P.replay("sp", e)

        @block.gpsimd
        def _(e):
            P.replay("pool", e)

        @block.scalar
        def _(e):
            P.replay("act", e)

        @block.vector
        def _(e):
            P.replay("dve", e)

        @block.tensor
        def _(e):
            P.replay("pe", e)
    return nc


def make_rtab(has_prev):
    kk = np.arange(128)[:, None].astype(np.float64)
    qq = np.arange(128)[None, :].astype(np.float64)
    cur = np.where(qq >= kk, -(qq - kk), NEG)
    prev_a = np.where(qq <= kk, -(128 + qq - kk), NEG)
    prev_b = np.where(qq < kk, -(128 + qq - kk), NEG)
    neg = np.full((128, 128), NEG)
    ha = prev_a if has_prev else neg
    hb = prev_b if has_prev else neg
    t = np.zeros((128, 6, 512), np.float32)
    t[:, 0] = np.tile(cur, (1, 4))
    t[:, 1] = np.tile(prev_a, (1, 4))
    t[:, 2] = np.tile(prev_b, (1, 4))
    t[:, 3] = np.concatenate([ha, prev_a, prev_a, prev_a], axis=1)
    t[:, 4] = np.tile(ha, (1, 4))
    t[:, 5] = np.concatenate([hb, prev_b, prev_b, prev_b], axis=1)
    return t


def make_in_maps(x, norm1_g, w_in, q_norm_a, k_norm_a, q_norm_b, k_norm_b, sinks_b, w_branch_a, w_branch_b,
                 w_out, norm2_g, w_ff1, w_ff2):
    f = lambda a: np.ascontiguousarray(np.asarray(a, dtype=np.float32))
    x = f(x)
    shared = {
        "w_in": f(w_in)[0], "norm1_g": f(norm1_g).reshape(1, D), "norm2_g": f(norm2_g).reshape(1, D),
        "gvec": np.ascontiguousarray(np.stack([f(q_norm_a)[0], f(k_norm_a)[0], f(q_norm_b)[0], f(k_norm_b)[0]],
                                              axis=1)),
        "sinks": f(sinks_b).reshape(1, 16),
        "w_bra": f(w_branch_a)[0], "w_brb": f(w_branch_b)[0], "w_out": f(w_out)[0],
        "w_ff1": f(w_ff1)[0], "w_ff2": f(w_ff2)[0],
        "ident": np.eye(128, dtype=np.float32).astype(ml_dtypes.bfloat16),
    }
    maps = []
    for c in range(N_CORES):
        b, s = divmod(c, 4)
        xe = np.zeros((TEXT, D), np.float32)
        xe[TOWN:] = x[b, s * TOWN:(s + 1) * TOWN]
        if s > 0:
            xe[:TOWN] = x[b, (s - 1) * TOWN:s * TOWN]
        m = dict(shared)
        m["xe"] = xe
        m["rtab"] = make_rtab(s > 0)
        maps.append(m)
    return maps


_CACHE = {}


def kernel(**inputs):
    if "nc" not in _CACHE:
        _CACHE["nc"] = build_program()
    nc = _CACHE["nc"]
    maps = make_in_maps(**inputs)
    res = run_bass_kernel_spmd(nc, maps, core_ids=list(range(N_CORES)))
    outp = np.zeros((2, 4 * TOWN, D), np.float32)
    for c in range(N_CORES):
        b, s = divmod(c, 4)
        outp[b, s * TOWN:(s + 1) * TOWN] = res.results[c]["out"]
    return outp
```

```python
import math
from contextlib import ExitStack

import numpy as np
import ml_dtypes

import concourse.bass as bass
import concourse.mybir as mybir
from concourse.bass_utils import run_bass_kernel_spmd

F32 = mybir.dt.float32
BF16 = mybir.dt.bfloat16
AF = mybir.ActivationFunctionType
ALU = mybir.AluOpType

D = 2048
NCH = 16
TOWN = 2048
TEXT = 4096
DFF = 8192
EPS = 1e-6
NEG = -1.0e9
N_CORES = 8

OFF_QA, OFF_KA, OFF_VA, OFF_QB, OFF_KB, OFF_VB, OFF_GA, OFF_GB = 0, 3072, 6144, 9216, 11264, 11520, 11776, 13824
IN_COLS = 15872


class Grp:
    def __init__(self, name, d, halo, nheads):
        self.name, self.d, self.halo, self.nheads = name, d, halo, nheads
        self.start = TOWN - halo
        self.L = (halo + TOWN) // d
        self.Lq = TOWN // d
        self.nbo = self.Lq // 128
        self.nb = self.nbo + 1
        self.nblk = d * self.nb
        self.ktot = halo + TOWN


GA0 = Grp("a0", 1, 128, 8)
GA1 = Grp("a1", 4, 512, 8)
GA2 = Grp("a2", 16, 2048, 8)
GB = Grp("b", 1, 128, 2)
AGRPS = [GA0, GA1, GA2]


def alibi_slopes():
    i = np.arange(1, 41, dtype=np.float32)
    return (2.0 ** (-8.0 * i / 40)).astype(np.float32)


class Sem:
    def __init__(self, h, name):
        self.h, self.name, self.count = h, name, 0


class Buf:
    def __init__(self, name):
        self.name = name
        self.w = []
        self.r = []
        self.dsem = None


class Prog:
    ENG = ("pe", "act", "dve", "pool", "sp")

    def __init__(self, nc):
        self.nc = nc
        self.es = ExitStack()
        self.q = {e: [] for e in self.ENG}
        self.prog_sem = {}
        self.all_dma_sems = []
        for e in ("pe", "act", "dve", "pool"):
            self.prog_sem[e] = self.sem("prog_" + e)
        self.nsem = 0

    def sem(self, name):
        h = self.es.enter_context(self.nc.semaphore(name))
        return Sem(h, name)

    def _deps(self, reads, writes, eng=None):
        waits = []
        for b in reads:
            waits += b.w
        for b in writes:
            waits += b.w
            waits += b.r
        if eng == "pe":
            me = self.prog_sem["pe"]
            waits = [w for w in waits if w[0] is not me]
        return waits

    def _commit(self, ev, reads, writes):
        for b in reads:
            b.r.append(ev)
        for b in writes:
            b.w = [ev]
            b.r = []

    def op(self, eng, fn, reads=(), writes=(), extra_waits=()):
        waits = self._deps(reads, writes, eng) + list(extra_waits)
        s = self.prog_sem[eng]
        s.count += 1
        ev = (s, s.count)
        self.q[eng].append((waits, fn, s, 1))
        self._commit(ev, reads, writes)
        return ev

    def group(self, eng, fns, reads=(), writes=(), extra_waits=()):
        waits = self._deps(reads, writes, eng) + list(extra_waits)
        s = self.prog_sem[eng]
        s.count += 1
        ev = (s, s.count)
        n = len(fns)
        for i, fn in enumerate(fns):
            self.q[eng].append((waits if i == 0 else [], fn, s if i == n - 1 else None, 1))
        self._commit(ev, reads, writes)
        return ev

    def dma(self, eng, fn, sem_owner, reads=(), writes=(), extra_waits=()):
        if sem_owner.dsem is None:
            sem_owner.dsem = self.sem("d_" + sem_owner.name)
            self.all_dma_sems.append(sem_owner.dsem)
        s = sem_owner.dsem
        waits = self._deps(reads, writes) + list(extra_waits)
        s.count += 16
        ev = (s, s.count)
        self.q[eng].append((waits, fn, s, 16))
        self._commit(ev, reads, writes)
        return ev

    def barrier(self, engines=ENG):
        evs = [(s, s.count) for s in self.prog_sem.values() if s.count > 0]
        evs += [(s, s.count) for s in self.all_dma_sems if s.count > 0]
        for e in engines:
            self.q[e].append((list(evs), None, None, 0))

    def replay(self, eng_name, eng):
        waited = {}
        for waits, fn, s, n in self.q[eng_name]:
            for ws, v in waits:
                if waited.get(ws.name, 0) < v:
                    eng.wait_ge(ws.h, v)
                    waited[ws.name] = v
            if fn is None:
                continue
            ins = fn(eng)
            if s is not None:
                ins.then_inc(s.h, n)


class Arena:
    def __init__(self, nc, limit):
        self.nc, self.off, self.limit, self.n = nc, 16512, limit, 0

    def alloc(self, name, shape, dtype):
        esz = 4 if dtype == F32 else 2
        nbytes = int(np.prod(shape[1:])) * esz
        nbytes = (nbytes + 63) // 64 * 64
        assert self.off + nbytes <= self.limit, f"SBUF overflow allocating {name}: {self.off}+{nbytes}>{self.limit}"
        self.n += 1
        t = self.nc.alloc_sbuf_tensor_at(f"{name}_{self.n}", list(shape), dtype, offset=self.off)
        self.off += nbytes
        return t

    def mark(self):
        return self.off

    def reset(self, m):
        self.off = m


def ap_of(t):
    return t if isinstance(t, bass.AP) else t[:]


def custom_ap(base_ap, extra_off, dims):
    a = ap_of(base_ap)
    return bass.AP(a.tensor, a.offset + extra_off, [list(a.ap[0])] + [list(x) for x in dims])


def build_program(upto=99, debug=False):
    nc = bass.Bass("TRN2", target_bir_lowering=False)
    okind = "ExternalOutput" if debug else "Internal"

    def dram(name, shape, dt, kind):
        return nc.dram_tensor(name, list(shape), dt, kind=kind)

    xe = dram("xe", [TEXT, D], F32, "ExternalInput").ap()
    w_in = dram("w_in", [D, IN_COLS], F32, "ExternalInput").ap()
    norm1_g = dram("norm1_g", [1, D], F32, "ExternalInput").ap()
    norm2_g = dram("norm2_g", [1, D], F32, "ExternalInput").ap()
    gvec_in = dram("gvec", [128, 4], F32, "ExternalInput").ap()
    sinks_in = dram("sinks", [1, 16], F32, "ExternalInput").ap()
    w_bra = dram("w_bra", [1024, D], F32, "ExternalInput").ap()
    w_brb = dram("w_brb", [D, D], F32, "ExternalInput").ap()
    w_out = dram("w_out", [D, D], F32, "ExternalInput").ap()
    w_ff1 = dram("w_ff1", [D, DFF], F32, "ExternalInput").ap()
    w_ff2 = dram("w_ff2", [DFF, D], F32, "ExternalInput").ap()
    ident_in = dram("ident", [128, 128], BF16, "ExternalInput").ap()
    rtab_in = dram("rtab", [128, 6, 512], F32, "ExternalInput").ap()
    out = dram("out", [TOWN, D], F32, "ExternalOutput").ap()

    qs = dram("qs", [40, 128, TOWN], BF16, okind).ap()
    ks = {g.name: dram("ks_" + g.name, [g.nheads, 128, g.ktot], BF16, okind).ap() for g in AGRPS + [GB]}
    vs = {g.name: dram("vs_" + g.name, [g.nheads, 128, g.nblk * 128], BF16, okind).ap() for g in AGRPS + [GB]}
    gs = dram("gs", [32, 128, TOWN], BF16, okind).ap()
    w1c = dram("w1c", [16, 128, 8192], BF16, "Internal").ap()
    w2c = dram("w2c", [8, 128, 16384], BF16, "Internal").ap()
    dbg_ot = dram("dbg_ot", [24, 128, TOWN], BF16, "ExternalOutput").ap() if debug else None
    dbg_mg = dram("dbg_mg", [16, 128, TOWN], BF16, "ExternalOutput").ap() if debug else None
    dbg_h2 = dram("dbg_h2", [16, 128, TOWN], BF16, "ExternalOutput").ap() if debug else None

    P = Prog(nc)
    arena = Arena(nc, 229376)
    psum = [nc.alloc_psum_tensor(f"bank{i}", [128, 512], F32) for i in range(8)]
    bank = [Buf(f"bank{i}") for i in range(8)]

    ident = arena.alloc("ident", [128, 128], BF16)
    ones = arena.alloc("ones", [128, 128], BF16)
    gvec = arena.alloc("gvec", [128, 4], F32)
    gvs = arena.alloc("gvs", [128, 4], F32)
    gv2 = arena.alloc("gv2", [128, 4], F32)
    esink = arena.alloc("esink", [128, 16], F32)
    B_const = Buf("consts")
    wring = [arena.alloc(f"wring{i}", [128, 8192], BF16) for i in range(2)]
    wbuf = [Buf(f"wring{i}") for i in range(2)]
    wctr = [0]
    persist_mark = arena.mark()

    def wslot():
        i = wctr[0] % len(wring)
        wctr[0] += 1
        return wring[i], wbuf[i]

    P.dma("sp", lambda e: e.dma_start(out=ident[:], in_=ident_in[:, :]), B_const, writes=[B_const])
    cb2 = Buf("consts2")
    P.dma("sp", lambda e: e.dma_start(out=gvec[:], in_=gvec_in[:, :]), cb2, writes=[cb2])
    cb3 = Buf("consts3")
    P.dma("sp", lambda e: e.dma_start(out=esink[:], in_=sinks_in.partition_broadcast(128)), cb3, writes=[cb3])
    B_ones = Buf("ones")
    P.op("dve", lambda e: e.memset(ones[:], 1.0), writes=[B_ones])
    B_gvs = Buf("gvs")
    P.op("dve", lambda e: e.tensor_scalar(out=gvs[:], in0=gvec[:], scalar1=math.sqrt(128.0), scalar2=None,
                                          op0=ALU.mult), reads=[cb2], writes=[B_gvs])

    B_gv2 = Buf("gv2")
    P.op("dve", lambda e: e.tensor_scalar(out=gv2[:], in0=gvec[:], scalar1=128.0 ** -0.5, scalar2=None,
                                          op0=ALU.mult), reads=[cb2], writes=[B_gv2])
    pool_hold = []
    bank_rr = [0]

    def next_bank(lo=0, hi=8):
        n = hi - lo
        i = lo + bank_rr[0] % n
        bank_rr[0] += 1
        return i

    m_phase01 = arena.mark()
    hT = arena.alloc("hT", [128, NCH, TEXT], BF16)
    B_hT = Buf("hT")
    m_p0 = arena.mark()

    def norm_transpose_phase(src_rows_ap, ntiles, gain_in, dstT, B_dst, ntok_total, tagp, lag=1):
        NS, NX = lag + 1, lag + 2
        g1b = arena.alloc(tagp + "gb", [128, D], F32)
        B_g1b = Buf(tagp + "gb")
        P.dma("sp", lambda e: e.dma_start(out=g1b[:], in_=gain_in.partition_broadcast(128)), B_g1b, writes=[B_g1b])
        xt = [arena.alloc(f"{tagp}xt{i}", [128, D], F32) for i in range(NX)]
        B_xt = [Buf(f"{tagp}xt{i}") for i in range(NX)]
        xs = [arena.alloc(f"{tagp}xs{i}", [128, D], BF16) for i in range(NS)]
        B_xs = [Buf(f"{tagp}xs{i}") for i in range(NS)]
        B_xs2 = [Buf(f"{tagp}xsb{i}") for i in range(NS)]
        junk = arena.alloc(tagp + "junk", [128, D], BF16)
        B_junk = Buf(tagp + "junk")
        st = [arena.alloc(f"{tagp}st{i}", [128, 4], F32) for i in range(NS)]
        B_st = [Buf(f"{tagp}st{i}") for i in range(NS)]
        first_loads = []

        def xload(j):
            x3 = j % NX
            ev = P.dma("sp", lambda e, j=j, x3=x3: e.dma_start(out=xt[x3][:],
                                                               in_=src_rows_ap[128 * j:128 * j + 128, :]),
                       B_xt[x3], writes=[B_xt[x3]])
            if j < 2:
                first_loads.append(ev)

        def stage_a(j):
            s = j % NS
            x3 = j % NX
            if j == 0:
                for jj in range(NX - 1):
                    xload(jj)
            if j + NX - 1 < ntiles:
                xload(j + NX - 1)
            P.op("act", lambda e, s=s, x3=x3: e.activation(out=junk[:], in_=xt[x3][:], func=AF.Square,
                                                          accum_out=st[s][:, 0:1]),
                 reads=[B_xt[x3]], writes=[B_junk, B_st[s]])
            P.op("act", lambda e, s=s: e.activation(out=st[s][:, 1:2], in_=st[s][:, 0:1], func=AF.Ln,
                                                    scale=1.0 / D, bias=EPS), reads=[], writes=[B_st[s]])
            P.op("act", lambda e, s=s: e.activation(out=st[s][:, 2:3], in_=st[s][:, 1:2], func=AF.Exp, scale=-0.5),
                 reads=[], writes=[B_st[s]])
            P.op("dve", lambda e, s=s, x3=x3: e.scalar_tensor_tensor(out=xs[s][:], in0=xt[x3][:],
                                                                      scalar=st[s][:, 2:3],
                                                                      in1=g1b[:], op0=ALU.mult, op1=ALU.mult),
                 reads=[B_xt[x3], B_st[s], B_g1b], writes=[B_xs[s]])

        def stage_b(j):
            s = j % NS
            for qd in range(4):
                bi = next_bank()
                pb = psum[bi][:].bitcast(BF16)
                fns = [(lambda e, c=4 * qd + k, k=k, pb=pb, s=s: e.transpose(out=pb[:, 128 * k:128 * k + 128],
                                                                           in_=xs[s][:, 128 * c:128 * c + 128],
                                                                           identity=ident[:])) for k in range(4)]
                P.group("pe", fns, reads=[B_xs[s], B_const], writes=[bank[bi]])
                dst = dstT[:, 4 * qd:4 * qd + 4, 128 * j:128 * j + 128]
                src = pb[:, 0:512].rearrange("p (a b) -> p a b", a=4)
                if qd == 0:
                    P.op("act", lambda e, dst=dst, src=src: e.copy(out=dst, in_=src), reads=[bank[bi]], writes=[B_dst])
                else:
                    P.op("dve", lambda e, dst=dst, src=src: e.tensor_copy(out=dst, in_=src), reads=[bank[bi]],
                         writes=[B_dst])

        for jj in range(lag):
            stage_a(jj)
        for j in range(ntiles):
            if j + lag < ntiles:
                stage_a(j + lag)
            stage_b(j)
        return first_loads

    norm_transpose_phase(xe, TEXT // 128, norm1_g, hT, B_hT, TEXT, "p0")
    arena.reset(m_p0)
    P.barrier(("pe", "act", "dve", "sp"))

    if upto >= 1:
        stg = [arena.alloc(f"stg{i}", [128, 2048], BF16) for i in range(3)]
        B_stg = [Buf(f"stg{i}") for i in range(3)]
        stg_ctr = [0]
        vst = [arena.alloc(f"vst{i}", [128, 4, 4, 128], BF16) for i in range(2)]
        B_vst = [[Buf(f"vst{i}_{j}") for j in range(4)] for i in range(2)]
        vst_ctr = [0]
        sq = [arena.alloc(f"sq{i}", [128, 512], BF16) for i in range(2)]
        B_sq = [Buf(f"sq{i}") for i in range(2)]
        lnb = [arena.alloc(f"lnb{i}", [128, 512], F32) for i in range(2)]
        B_lnb = [Buf(f"lnb{i}") for i in range(2)]
        rb = [arena.alloc(f"rb{i}", [128, 512], F32) for i in range(2)]
        B_rb = [Buf(f"rb{i}") for i in range(2)]
        ep_ctr = [0]
        w_in_v = w_in.rearrange("(k p) n -> p k n", p=128)

        def tok_ap(g, c, tau0, n, own_only):
            L = g.Lq if own_only else g.L
            base = TOWN if own_only else g.start
            r0, u0 = divmod(tau0, L)
            if g.d == 1:
                return hT[:, c, base + tau0: base + tau0 + n]
            if u0 + n <= L:
                e0 = base + g.d * u0 + r0
                return custom_ap(hT[:, c, :], e0, [[g.d, n]])
            assert u0 == 0 and n % L == 0
            nr = n // L
            e0 = base + r0
            return custom_ap(hT[:, c, :], e0, [[1, nr], [g.d, L]])

        pending = []

        def flush_pending(keep=0):
            while len(pending) > keep:
                pending.pop(0)()

        def qk_tile(wt, wb, ci, g, own_only, gcol, gscaled, dst_ap):
            total = TOWN if own_only else g.ktot
            L = g.Lq if own_only else g.L
            if g.d == 1:
                blocks = [(t0, min(512, total - t0)) for t0 in range(0, total, 512)]
            elif L >= 512:
                assert L % 512 == 0 or L == 640
                if L == 640:
                    blocks = [(r * L + u0, 320) for r in range(g.d) for u0 in (0, 320)]
                else:
                    blocks = [(r * L + u0, 512) for r in range(g.d) for u0 in range(0, L, 512)]
            else:
                blocks = [(t0, 512) for t0 in range(0, total, 512)]
            gsrc = gvs if gscaled else gvec
            cur = {"slot": None, "c0": 0, "n": 0}

            def flush_stage():
                if cur["slot"] is None or cur["n"] == 0:
                    return
                s, c0, n = cur["slot"], cur["c0"], cur["n"]
                P.dma("sp", lambda e: e.dma_start(out=dst_ap[:, c0:c0 + n], in_=stg[s][:, 0:n]), B_stg[s],
                      reads=[B_stg[s]])
                cur["slot"] = None

            for (t0, n) in blocks:
                if cur["slot"] is None or cur["n"] + n > 2048:
                    pending.append(lambda f=flush_stage_snapshot(cur, dst_ap): f())
                    cur["slot"] = stg_ctr[0] % 3
                    stg_ctr[0] += 1
                    cur["c0"] = t0
                    cur["n"] = 0
                s, off = cur["slot"], cur["n"]
                cur["n"] += n
                bi = next_bank(0, 5)
                fns = []
                for k in range(NCH):
                    fns.append(lambda e, k=k, t0=t0, n=n, bi=bi: e.matmul(
                        psum[bi][:, 0:n], lhsT=wt[:, k * 512 + ci * 128: k * 512 + ci * 128 + 128],
                        rhs=tok_ap(g, k, t0, n, own_only), start=(k == 0), stop=(k == NCH - 1)))
                P.group("pe", fns, reads=[wb, B_hT], writes=[bank[bi]])
                es = ep_ctr[0] % 2
                ep_ctr[0] += 1
                P.op("act", lambda e, bi=bi, n=n, es=es: e.activation(out=sq[es][:, 0:n], in_=psum[bi][:, 0:n],
                                                                    func=AF.Square),
                     reads=[bank[bi]], writes=[B_sq[es]])

                def epilogue(bi=bi, n=n, es=es, s=s, off=off):
                    b2 = 5 + next_bank(0, 3)
                    P.group("pe", [lambda e: e.matmul(psum[b2][:, 0:n], lhsT=ones[:], rhs=sq[es][:, 0:n],
                                                      start=True, stop=True)],
                            reads=[B_sq[es], B_ones], writes=[bank[b2]])
                    P.op("act", lambda e: e.activation(out=lnb[es][:, 0:n], in_=psum[b2][:, 0:n], func=AF.Ln,
                                                       bias=128.0 * EPS), reads=[bank[b2]], writes=[B_lnb[es]])
                    P.op("act", lambda e: e.activation(out=rb[es][:, 0:n], in_=lnb[es][:, 0:n], func=AF.Exp,
                                                       scale=-0.5), reads=[B_lnb[es]], writes=[B_rb[es]])
                    P.op("dve", lambda e: e.scalar_tensor_tensor(out=stg[s][:, off:off + n], in0=psum[bi][:, 0:n],
                                                                 scalar=gsrc[:, gcol:gcol + 1], in1=rb[es][:, 0:n],
                                                                 op0=ALU.mult, op1=ALU.mult),
                         reads=[bank[bi], B_rb[es], B_gvs, cb2], writes=[B_stg[s]])

                pending.append(epilogue)
                flush_pending(keep=1)
            pending.append(lambda f=flush_stage_snapshot(cur, dst_ap): f())

        def flush_stage_snapshot(cur, dst_ap):
            s, c0, n = cur["slot"], cur["c0"], cur["n"]

            def f():
                if s is None or n == 0:
                    return
                P.dma("sp", lambda e: e.dma_start(out=dst_ap[:, c0:c0 + n], in_=stg[s][:, 0:n]), B_stg[s],
                      reads=[B_stg[s]])
            return f

        def v_slab(wt, wb, col0, ncols, g, h0):
            nh = ncols // 128
            vsd = vs[g.name]
            bb = 0
            slot = None
            b0 = 0
            for b in range(g.nblk):
                r, jb = divmod(b, g.nb)
                if slot is None:
                    slot = vst_ctr[0] % 2
                    vst_ctr[0] += 1
                    bb = 0
                    b0 = b
                e0 = g.start + g.d * 128 * jb + r
                bi = next_bank(0, 5)
                fns = []
                for k in range(NCH):
                    lhs = hT[:, k, e0:e0 + 128] if g.d == 1 else custom_ap(hT[:, k, :], e0, [[g.d, 128]])
                    fns.append(lambda e, k=k, lhs=lhs, bi=bi: e.matmul(
                        psum[bi][:, 0:ncols], lhsT=lhs, rhs=wt[:, k * 512 + col0: k * 512 + col0 + ncols],
                        start=(k == 0), stop=(k == NCH - 1)))
                P.group("pe", fns, reads=[wb, B_hT], writes=[bank[bi]])
                dst = vst[slot][:, 0:nh, bb, :]
                src = psum[bi][:, 0:ncols].rearrange("p (a b) -> p a b", a=nh)
                if b % 2 == 0:
                    P.op("act", lambda e, dst=dst, src=src: e.copy(out=dst, in_=src), reads=[bank[bi]],
                         writes=[B_vst[slot][bb]])
                else:
                    P.op("dve", lambda e, dst=dst, src=src: e.tensor_copy(out=dst, in_=src), reads=[bank[bi]],
                         writes=[B_vst[slot][bb]])
                bb += 1
                if bb == 4 or b == g.nblk - 1:
                    for hh in range(nh):
                        P.dma("sp", lambda e, slot=slot, hh=hh, b0=b0, bb=bb: e.dma_start(
                            out=vsd[h0 + hh, :, 128 * b0:128 * (b0 + bb)],
                            in_=vst[slot][:, hh, 0:bb, :].rearrange("p a b -> p (a b)")),
                            B_vst[slot][hh], reads=B_vst[slot][0:bb])
                    slot = None

        def gate_tile(wt, wb, ci, gidx):
            s = stg_ctr[0] % 3
            stg_ctr[0] += 1
            for tb in range(4):
                bi = next_bank(0, 5)
                fns = []
                for k in range(NCH):
                    fns.append(lambda e, k=k, tb=tb, bi=bi: e.matmul(
                        psum[bi][:, :], lhsT=wt[:, k * 512 + ci * 128: k * 512 + ci * 128 + 128],
                        rhs=hT[:, k, TOWN + 512 * tb: TOWN + 512 * tb + 512], start=(k == 0), stop=(k == NCH - 1)))
                P.group("pe", fns, reads=[wb, B_hT], writes=[bank[bi]])
                P.op("act", lambda e, tb=tb, bi=bi, s=s: e.activation(out=stg[s][:, 512 * tb:512 * tb + 512],
                                                                    in_=psum[bi][:, :], func=AF.Sigmoid),
                     reads=[bank[bi]], writes=[B_stg[s]])
            P.dma("sp", lambda e, s=s: e.dma_start(out=gs[gidx, :, :], in_=stg[s][:, :]), B_stg[s], reads=[B_stg[s]])


        xn = [arena.alloc(f"xn{i}", [128, 4, 128], BF16) for i in range(2)]
        B_xn = [Buf(f"xn{i}") for i in range(2)]
        jk = arena.alloc("jk", [128, 128], BF16)
        B_jk = Buf("jk")
        sst = [arena.alloc(f"sst{i}", [128, 12], F32) for i in range(2)]
        B_sst = [Buf(f"sst{i}") for i in range(2)]
        tm_ctr = [0]

        def qk_slab_tm(wt, wb, g, h0, is_k, dst_heads):
            nblk = g.nblk if is_k else g.d * g.nbo
            per = g.nb if is_k else g.nbo
            base = g.start if is_k else TOWN
            gsc = gv2[:, 1:2] if is_k else gvec[:, 0:1]
            state = {"slot": None, "bb": 0, "b0": 0}
            for b in range(nblk):
                r, jb = divmod(b, per)
                e0 = base + g.d * 128 * jb + r
                bi = next_bank(0, 5)
                fns = []
                for k in range(NCH):
                    lhs = custom_ap(hT[:, k, :], e0, [[g.d, 128]])
                    fns.append(lambda e, k=k, lhs=lhs, bi=bi: e.matmul(
                        psum[bi][:, :], lhsT=lhs, rhs=wt[:, k * 512: k * 512 + 512],
                        start=(k == 0), stop=(k == NCH - 1)))
                P.group("pe", fns, reads=[wb, B_hT], writes=[bank[bi]])
                ts_ = tm_ctr[0] % 2
                tm_ctr[0] += 1
                for hh in range(4):
                    P.op("act", lambda e, bi=bi, hh=hh, ts_=ts_: e.activation(
                        out=jk[:], in_=psum[bi][:, 128 * hh:128 * hh + 128], func=AF.Square,
                        accum_out=sst[ts_][:, hh:hh + 1]), reads=[bank[bi]], writes=[B_jk, B_sst[ts_]])
                P.op("act", lambda e, ts_=ts_: e.activation(out=sst[ts_][:, 4:8], in_=sst[ts_][:, 0:4], func=AF.Ln,
                                                            scale=1.0 / 128, bias=EPS), reads=[], writes=[B_sst[ts_]])
                P.op("act", lambda e, ts_=ts_: e.activation(out=sst[ts_][:, 8:12], in_=sst[ts_][:, 4:8], func=AF.Exp,
                                                            scale=-0.5), reads=[], writes=[B_sst[ts_]])
                P.op("dve", lambda e, bi=bi, ts_=ts_: e.tensor_tensor(
                    out=xn[ts_][:], in0=psum[bi][:, :].rearrange("p (a b) -> p a b", a=4),
                    in1=custom_ap(sst[ts_], 8, [[1, 4], [0, 128]]), op=ALU.mult),
                    reads=[bank[bi], B_sst[ts_]], writes=[B_xn[ts_]])
                if state["slot"] is None:
                    state["slot"] = vst_ctr[0] % 2
                    vst_ctr[0] += 1
                    state["bb"] = 0
                    state["b0"] = b
                slot, bb, b0 = state["slot"], state["bb"], state["b0"]
                state["bb"] += 1
                last = (state["bb"] == 4 or b == nblk - 1)
                if last:
                    state["slot"] = None

                def epilogue(ts_=ts_, slot=slot, bb=bb, b0=b0, last=last, b=b):
                    b2 = 5 + next_bank(0, 3)
                    pb = psum[b2][:].bitcast(BF16)
                    fns = [(lambda e, hh=hh: e.transpose(out=pb[:, 128 * hh:128 * hh + 128], in_=xn[ts_][:, hh, :],
                                                         identity=ident[:])) for hh in range(4)]
                    P.group("pe", fns, reads=[B_xn[ts_], B_const], writes=[bank[b2]])
                    dst = vst[slot][:, 0:4, bb, :]
                    src = pb[:, 0:512].rearrange("p (a b) -> p a b", a=4)
                    if b % 2 == 0:
                        P.op("act", lambda e: e.activation(out=dst, in_=src, func=AF.Copy, scale=gsc),
                             reads=[bank[b2], cb2, B_gv2], writes=[B_vst[slot][bb]])
                    else:
                        P.op("dve", lambda e: e.tensor_scalar(out=dst, in0=src, scalar1=gsc, scalar2=None,
                                                              op0=ALU.mult),
                             reads=[bank[b2], cb2, B_gv2], writes=[B_vst[slot][bb]])
                    if last:
                        nbb = bb + 1
                        for hh in range(4):
                            P.dma("sp", lambda e, hh=hh: e.dma_start(
                                out=dst_heads[h0 + hh][:, 128 * b0:128 * (b0 + nbb)],
                                in_=vst[slot][:, hh, 0:nbb, :].rearrange("p a b -> p (a b)")),
                                B_vst[slot][hh], reads=B_vst[slot][0:nbb])

                pending.append(epilogue)
                flush_pending(keep=1)
            flush_pending()

        nslab = IN_COLS // 512
        B_cache = Buf("wcache")
        ff1_vc = w_ff1.rearrange("(k p) n -> p k n", p=128)
        ff2_vc = w_ff2.rearrange("(j p) m -> p j m", p=128)
        for sl in range(nslab):
            wt, wb = wslot()
            P.dma("pool", lambda e, sl=sl, wt=wt: e.dma_start(
                out=wt[:].rearrange("p (k n) -> p k n", k=NCH), in_=w_in_v[:, :, 512 * sl:512 * sl + 512]),
                wb, writes=[wb])
            if sl < 16:
                P.dma("pool", lambda e, sl=sl: e.dma_start(
                    out=w1c[sl].rearrange("p (k n) -> p k n", k=NCH), in_=ff1_vc[:, :, 512 * sl:512 * sl + 512]),
                    B_cache, writes=[B_cache])
            elif sl < 24:
                ci_ = sl - 16
                P.dma("pool", lambda e, ci_=ci_: e.dma_start(
                    out=w2c[ci_].rearrange("p (j m) -> p j m", j=32),
                    in_=ff2_vc[:, 32 * (ci_ % 2):32 * (ci_ % 2) + 32, 512 * (ci_ // 2):512 * (ci_ // 2) + 512]),
                    B_cache, writes=[B_cache])
            for ci in range(4):
                cc = 4 * sl + ci
                col = 128 * cc
                if col < OFF_KA:
                    gi, h = divmod(cc, 8)
                    if gi == 0:
                        qk_tile(wt, wb, ci, AGRPS[gi], True, 0, True, qs[gi * 8 + h])
                    elif ci == 0:
                        flush_pending()
                        qk_slab_tm(wt, wb, AGRPS[gi], h, False, [qs[gi * 8 + x] for x in range(8)])
                elif col < OFF_VA:
                    gi, h = divmod(cc - 24, 8)
                    if gi == 0:
                        qk_tile(wt, wb, ci, AGRPS[gi], False, 1, False, ks[AGRPS[gi].name][h])
                    elif ci == 0:
                        flush_pending()
                        qk_slab_tm(wt, wb, AGRPS[gi], h, True, [ks[AGRPS[gi].name][x] for x in range(8)])
                elif col < OFF_QB:
                    if ci == 0:
                        flush_pending()
                        gi, hh = divmod(sl - 12, 2)
                        v_slab(wt, wb, 0, 512, AGRPS[gi], 4 * hh)
                elif col < OFF_KB:
                    j = cc - 72
                    qk_tile(wt, wb, ci, GB, True, 2, True, qs[24 + j])
                elif col < OFF_VB:
                    qk_tile(wt, wb, ci, GB, False, 3, False, ks["b"][cc - 88])
                elif col < OFF_GA:
                    if ci == 2:
                        flush_pending()
                        v_slab(wt, wb, 256, 256, GB, 0)
                else:
                    flush_pending()
                    gate_tile(wt, wb, ci, cc - 92)
        flush_pending()

    arena.reset(m_phase01)
    P.barrier()


    slopes = alibi_slopes()
    if upto >= 2:
        oT = arena.alloc("oT", [128, 24, TOWN], BF16)
        B_oT = Buf("oT")
        m_p2 = arena.mark()
        rtab = arena.alloc("rtab", [128, 6, 512], F32)
        B_rtab = Buf("rtab")
        P.dma("sp", lambda e: e.dma_start(out=rtab[:], in_=rtab_in[:, :, :]), B_rtab, writes=[B_rtab])
        Vt = [arena.alloc(f"Vt{i}", [128, 32, 128], BF16) for i in range(2)]
        B_Vt = [Buf(f"Vt{i}") for i in range(2)]
        Qt = [wring[i][:, 0:2048] for i in range(2)]
        Kt = [wring[i][:, 2048:6144] for i in range(2)]
        PT = [wring[0][:, 6144:7168], wring[1][:, 6144:7168], wring[0][:, 7168:8192]]
        B_Qt = [Buf(f"Qt{i}") for i in range(2)]
        B_Kt = [Buf(f"Kt{i}") for i in range(2)]
        B_PTa = [Buf(f"PTa{i}") for i in range(3)]
        B_PTb = [Buf(f"PTb{i}") for i in range(3)]
        Oacc = arena.alloc("Oacc", [128, TOWN], F32)
        Dacc = arena.alloc("Dacc", [128, TOWN], F32)
        B_Oc = [Buf(f"Oacc{c}") for c in range(4)]
        B_Dc = [Buf(f"Dacc{c}") for c in range(4)]
        Et = [[arena.alloc(f"E{i}_{j}", [128, 512], F32) for j in range(3)] for i in range(2)]
        B_Et = [[Buf(f"E{i}_{j}") for j in range(3)] for i in range(2)]
        Pexp = [arena.alloc(f"Pexp{i}", [128, 1024], F32) for i in range(3)]
        B_Pexp = [[Buf(f"Pexp{i}_{j}") for j in range(2)] for i in range(3)]
        rec = [arena.alloc(f"rec{i}", [128, 512], F32) for i in range(2)]
        B_rec = [Buf(f"rec{i}") for i in range(2)]
        esx = arena.alloc("esx", [128, 16], F32)
        B_esx = Buf("esx")
        P.op("act", lambda e: e.activation(out=esx[:], in_=esink[:], func=AF.Exp), reads=[cb3], writes=[B_esx])

        ctr = {"unit": 0, "prep": 0, "fill": 0, "s": 0, "o": 0, "d": 0, "rec": 0}
        pend2 = []

        def flush2():
            while pend2:
                pend2.pop(0)()

        fin = {"slot": None}

        def finalize_chunk(h, c):
            cs = slice(512 * c, 512 * c + 512)
            P.op("act", lambda e, cs=cs: e.activation(out=Dacc[:, cs], in_=Dacc[:, cs], func=AF.Ln), reads=[],
                 writes=[B_Dc[c]])
            P.op("act", lambda e, cs=cs: e.activation(out=Dacc[:, cs], in_=Dacc[:, cs], func=AF.Exp, scale=-1.0),
                 reads=[], writes=[B_Dc[c]])
            P.op("dve", lambda e, h=h, cs=cs: e.tensor_tensor(out=oT[:, h, cs], in0=Oacc[:, cs], in1=Dacc[:, cs],
                                                              op=ALU.mult),
                 reads=[B_Oc[c], B_Dc[c]], writes=[B_oT])

        def prep_unit(g, kvh, qi, a, is_b, slot, first_group, bj):
            u = ctr["prep"] % 2
            ctr["prep"] += 1
            P.dma("sp", lambda e: e.dma_start(out=Qt[u], in_=qs[qi]), B_Qt[u], writes=[B_Qt[u]])
            P.dma("sp", lambda e: e.dma_start(out=Kt[u][:, 0:g.ktot], in_=ks[g.name][kvh]), B_Kt[u], writes=[B_Kt[u]])
            P.dma("sp", lambda e: e.dma_start(out=Vt[u][:, 0:g.nblk, :].rearrange("p a b -> p (a b)"),
                                              in_=vs[g.name][kvh]), B_Vt[u], writes=[B_Vt[u]])
            hidx = 5 if is_b else (4 if g.d == 16 else 3)
            pidx = 2 if is_b else 1
            P.op("act", lambda e: e.activation(out=Et[u][0][:], in_=rtab[:, 0, :], func=AF.Exp, scale=float(a)),
                 reads=[B_rtab], writes=[B_Et[u][0]])
            P.op("act", lambda e: e.activation(out=Et[u][1][:], in_=rtab[:, hidx, :], func=AF.Exp, scale=float(a)),
                 reads=[B_rtab], writes=[B_Et[u][1]])
            if g.d == 1:
                P.op("act", lambda e: e.activation(out=Et[u][2][:], in_=rtab[:, pidx, :], func=AF.Exp,
                                                   scale=float(a)), reads=[B_rtab], writes=[B_Et[u][2]])

        def attn_unit(g, kvh, qi, a, is_b, slot, first_group, bj, next_unit=None):
            u = ctr["unit"] % 2
            ctr["unit"] += 1
            def od_part(f, fp, ent):
                ob = 4 + ctr["o"] % 2
                ctr["o"] += 1
                db = 6 + ctr["d"] % 2
                ctr["d"] += 1
                fns = []
                for j, (qc, bp) in enumerate(ent):
                    fns.append(lambda e, j=j, bp=bp, ob=ob, fp=fp: e.matmul(
                        psum[ob][:, 128 * j:128 * j + 128], lhsT=Vt[u][:, bp, :], rhs=PT[fp][:, 128 * j:128 * j + 128],
                        start=True, stop=False))
                    fns.append(lambda e, j=j, bp=bp, ob=ob, fp=fp: e.matmul(
                        psum[ob][:, 128 * j:128 * j + 128], lhsT=Vt[u][:, bp + 1, :],
                        rhs=PT[fp][:, 512 + 128 * j:512 + 128 * j + 128], start=False, stop=True))
                P.group("pe", fns, reads=[B_Vt[u], B_PTa[fp], B_PTb[fp]], writes=[bank[ob]])
                fns = [lambda e, db=db, fp=fp: e.matmul(psum[db][:, :], lhsT=ones[:], rhs=PT[fp][:, 0:512],
                                                        start=True, stop=False),
                       lambda e, db=db, fp=fp: e.matmul(psum[db][:, :], lhsT=ones[:], rhs=PT[fp][:, 512:1024],
                                                        start=False, stop=True)]
                P.group("pe", fns, reads=[B_ones, B_PTa[fp], B_PTb[fp]], writes=[bank[db]])
                if is_b:
                    rs_ = ctr["rec"] % 2
                    ctr["rec"] += 1
                    P.op("act", lambda e, db=db, rs_=rs_: e.activation(
                        out=rec[rs_][:], in_=psum[db][:, :], func=AF.Ln, bias=esx[:, bj:bj + 1]),
                        reads=[bank[db], B_esx], writes=[B_rec[rs_]])
                    P.op("act", lambda e, rs_=rs_: e.activation(out=rec[rs_][:], in_=rec[rs_][:], func=AF.Exp,
                                                                scale=-1.0), reads=[], writes=[B_rec[rs_]])
                    P.op("dve", lambda e, ob=ob, rs_=rs_, f=f: e.tensor_tensor(
                        out=oT[:, slot, 512 * f:512 * f + 512], in0=psum[ob][:, :], in1=rec[rs_][:], op=ALU.mult),
                        reads=[bank[ob], B_rec[rs_]], writes=[B_oT])
                else:
                    if g.d == 1:
                        ov, dv, po, pd = (Oacc[:, 512 * f:512 * f + 512], Dacc[:, 512 * f:512 * f + 512],
                                          psum[ob][:, :], psum[db][:, :])
                    elif g.d == 4:
                        ov, dv, po, pd = Oacc[:, f:TOWN:4], Dacc[:, f:TOWN:4], psum[ob][:, :], psum[db][:, :]
                    else:
                        ov = custom_ap(Oacc, 4 * f, [[1, 4], [16, 128]])
                        dv = custom_ap(Dacc, 4 * f, [[1, 4], [16, 128]])
                        po = psum[ob][:, :].rearrange("p (j i) -> p j i", j=4)
                        pd = psum[db][:, :].rearrange("p (j i) -> p j i", j=4)
                    if first_group:
                        if fin["slot"] is not None:
                            finalize_chunk(fin["slot"], f)
                            if f == 3:
                                fin["slot"] = None
                        P.op("act", lambda e, ov=ov, po=po: e.copy(out=ov, in_=po), reads=[bank[ob]], writes=[B_Oc[f]])
                        P.op("dve", lambda e, dv=dv, pd=pd: e.tensor_copy(out=dv, in_=pd), reads=[bank[db]],
                             writes=[B_Dc[f]])
                    else:
                        P.op("dve", lambda e, ov=ov, po=po: e.tensor_tensor(out=ov, in0=po, in1=ov, op=ALU.add),
                             reads=[bank[ob]], writes=B_Oc)
                        P.op("dve", lambda e, dv=dv, pd=pd: e.tensor_tensor(out=dv, in0=pd, in1=dv, op=ALU.add),
                             reads=[bank[db]], writes=B_Dc)

            for f in range(4):
                if g.d == 1:
                    ent = [(128 * (4 * f + j), 4 * f + j) for j in range(4)]
                    eprev = 1 if f == 0 else 2
                elif g.d == 4:
                    ent = [(128 * (4 * f + j), 5 * f + j) for j in range(4)]
                    eprev = 1
                else:
                    ent = [(128 * (4 * f + j), 2 * (4 * f + j)) for j in range(4)]
                    eprev = 1
                fp = ctr["fill"] % 3
                ctr["fill"] += 1
                sp_, sc_ = ctr["s"] % 4, (ctr["s"] + 1) % 4
                ctr["s"] += 2
                for sb_, role in ((sp_, 0), (sc_, 1)):
                    fns = [(lambda e, j=j, qc=qc, bp=bp, sb_=sb_, role=role: e.matmul(
                        psum[sb_][:, 128 * j:128 * j + 128], lhsT=Kt[u][:, 128 * (bp + role):128 * (bp + role) + 128],
                        rhs=Qt[u][:, qc:qc + 128], start=True, stop=True)) for j, (qc, bp) in enumerate(ent)]
                    P.group("pe", fns, reads=[B_Qt[u], B_Kt[u]], writes=[bank[sb_]])
                    P.op("act", lambda e, sb_=sb_, role=role, fp=fp: e.activation(
                        out=Pexp[fp][:, 512 * role:512 * role + 512], in_=psum[sb_][:, :], func=AF.Exp),
                        reads=[bank[sb_]], writes=[B_Pexp[fp][role]])
                P.op("dve" if (is_b or first_group) else "pool", lambda e, fp=fp, eprev=eprev: e.tensor_tensor(
                    out=PT[fp][:, 0:512], in0=Pexp[fp][:, 0:512], in1=Et[u][eprev][:], op=ALU.mult),
                    reads=[B_Pexp[fp][0], B_Et[u][eprev]], writes=[B_PTa[fp]])
                P.op("pool", lambda e, fp=fp: e.tensor_tensor(
                    out=PT[fp][:, 512:1024], in0=Pexp[fp][:, 512:1024], in1=Et[u][0][:], op=ALU.mult),
                    reads=[B_Pexp[fp][1], B_Et[u][0]], writes=[B_PTb[fp]])
                pend2.append(lambda f=f, fp=fp, ent=ent: od_part(f, fp, ent))
                while len(pend2) > 2:
                    pend2.pop(0)()
                if f == 1 and next_unit is not None:
                    prep_unit(*next_unit)

        units = []
        for h in range(8):
            for gi, g in enumerate(AGRPS):
                units.append((g, h, gi * 8 + h, slopes[16 + 8 * gi + h] * g.d, False, h, gi == 0, 0))
        for j in range(16):
            units.append((GB, j // 8, 24 + j, slopes[j], True, 8 + j, False, j))
        prep_unit(*units[0])
        for i, un in enumerate(units):
            attn_unit(*un, next_unit=(units[i + 1] if i + 1 < len(units) else None))
            if (not un[4]) and un[0] is GA2:
                if un[1] < 7:
                    fin["slot"] = un[1]
                else:
                    flush2()
                    for c in range(4):
                        finalize_chunk(7, c)
        flush2()
        if debug:
            P.dma("sp", lambda e: e.dma_start(out=dbg_ot.rearrange("c p t -> p c t"), in_=oT[:]), B_oT, reads=[B_oT])
        arena.reset(m_p2)
        P.barrier()

    if upto >= 3:
        mT = arena.alloc("mT", [128, NCH, TOWN], BF16)
        B_mT = Buf("mT")
        m_p3 = arena.mark()
        gat = [[arena.alloc(f"gat{i}_{j}", [128, 512], BF16) for j in range(2)] for i in range(3)]
        B_gat = [[Buf(f"gat{i}_{j}") for j in range(2)] for i in range(3)]
        tmp = [arena.alloc(f"tmp{j}", [128, 512], F32) for j in range(2)]
        B_tmp = [Buf(f"tmp{j}") for j in range(2)]
        B_w2 = [Buf("wring0b"), Buf("wring1b")]
        bra_v = w_bra.rearrange("(k p) n -> p k n", p=128)
        brb_v = w_brb.rearrange("(k p) n -> p k n", p=128)
        gctr = 0
        for n in range(NCH):
            wi = wctr[0] % 2
            wt, wb = wslot()
            wb2 = B_w2[wi]
            P.dma("pool", lambda e, wt=wt, n=n: e.dma_start(
                out=wt[:, 0:1024].rearrange("p (k n) -> p k n", k=8), in_=bra_v[:, :, 128 * n:128 * n + 128]),
                wb, writes=[wb])
            P.dma("pool", lambda e, wt=wt, n=n: e.dma_start(
                out=wt[:, 1024:3072].rearrange("p (k n) -> p k n", k=16), in_=brb_v[:, :, 128 * n:128 * n + 128]),
                wb2, writes=[wb2])
            for tb in range(4):
                gsl = gctr % 3
                gctr += 1
                for j in range(2):
                    P.dma("sp", lambda e, gsl=gsl, j=j, n=n, tb=tb: e.dma_start(
                        out=gat[gsl][j][:], in_=gs[16 * j + n, :, 512 * tb:512 * tb + 512]),
                        B_gat[gsl][j], writes=[B_gat[gsl][j]])
                b1 = next_bank()
                b2 = next_bank()
                fns = [(lambda e, k=k, tb=tb, b1=b1, wt=wt: e.matmul(
                    psum[b1][:, :], lhsT=wt[:, 128 * k:128 * k + 128], rhs=oT[:, k, 512 * tb:512 * tb + 512],
                    start=(k == 0), stop=(k == 7))) for k in range(8)]
                P.group("pe", fns, reads=[wb, B_oT], writes=[bank[b1]])
                fns = [(lambda e, k=k, tb=tb, b2=b2, wt=wt: e.matmul(
                    psum[b2][:, :], lhsT=wt[:, 1024 + 128 * k:1024 + 128 * k + 128],
                    rhs=oT[:, 8 + k, 512 * tb:512 * tb + 512], start=(k == 0), stop=(k == 15))) for k in range(16)]
                P.group("pe", fns, reads=[wb2, wb, B_oT], writes=[bank[b2]])
                P.op("dve", lambda e, b1=b1, gsl=gsl: e.tensor_tensor(out=tmp[0][:], in0=psum[b1][:, :],
                                                                     in1=gat[gsl][0][:], op=ALU.mult),
                     reads=[bank[b1], B_gat[gsl][0]], writes=[B_tmp[0]])
                P.op("dve", lambda e, b2=b2, gsl=gsl: e.tensor_tensor(out=tmp[1][:], in0=psum[b2][:, :],
                                                                     in1=gat[gsl][1][:], op=ALU.mult),
                     reads=[bank[b2], B_gat[gsl][1]], writes=[B_tmp[1]])
                P.op("dve", lambda e, n=n, tb=tb: e.tensor_tensor(out=mT[:, n, 512 * tb:512 * tb + 512],
                                                                  in0=tmp[0][:], in1=tmp[1][:], op=ALU.add),
                     reads=[B_tmp[0], B_tmp[1]], writes=[B_mT])
        if debug:
            P.dma("sp", lambda e: e.dma_start(out=dbg_mg.rearrange("c p t -> p c t"), in_=mT[:]), B_mT, reads=[B_mT])
        arena.reset(m_p3)
        P.barrier(("pe", "act", "dve", "sp"))

    if upto >= 4:
        NXO = 6
        xo = [arena.alloc(f"xo{i}", [128, 512], F32) for i in range(NXO)]
        B_xo = [Buf(f"xo{i}") for i in range(NXO)]
        xctr = 0
        wout_v = w_out.rearrange("(k p) n -> p k n", p=128)
        its = [(ms, tt) for ms in range(4) for tt in range(16)]

        def xload(i):
            ms, tt = its[i]
            P.dma("sp", lambda e, i=i, tt=tt, ms=ms: e.dma_start(
                out=xo[i % NXO][:], in_=xe[TOWN + 128 * tt:TOWN + 128 * tt + 128, 512 * ms:512 * ms + 512]),
                B_xo[i % NXO], writes=[B_xo[i % NXO]])

        XPF = 3
        for i in range(XPF):
            xload(i)
        wt = wb = None
        for i, (ms, tt) in enumerate(its):
            if tt == 0:
                wt, wb = wslot()
                P.dma("pool", lambda e, wt=wt, ms=ms: e.dma_start(
                    out=wt[:].rearrange("p (k n) -> p k n", k=NCH), in_=wout_v[:, :, 512 * ms:512 * ms + 512]),
                    wb, writes=[wb])
            if i + XPF < len(its):
                xload(i + XPF)
            xs_ = i % NXO
            bi = next_bank()
            fns = [(lambda e, k=k, tt=tt, bi=bi, wt=wt: e.matmul(
                psum[bi][:, :], lhsT=mT[:, k, 128 * tt:128 * tt + 128], rhs=wt[:, 512 * k:512 * k + 512],
                start=(k == 0), stop=(k == NCH - 1))) for k in range(NCH)]
            P.group("pe", fns, reads=[wb, B_mT], writes=[bank[bi]])
            P.op("dve", lambda e, xs_=xs_, bi=bi: e.tensor_tensor(out=xo[xs_][:], in0=psum[bi][:, :],
                                                                 in1=xo[xs_][:], op=ALU.add),
                 reads=[bank[bi]], writes=[B_xo[xs_]])
            P.dma("sp", lambda e, xs_=xs_, tt=tt, ms=ms: e.dma_start(
                out=out[128 * tt:128 * tt + 128, 512 * ms:512 * ms + 512], in_=xo[xs_][:]),
                B_xo[xs_], reads=[B_xo[xs_]])
        P.barrier()
        arena.reset(persist_mark)

    if upto >= 5:
        h2T = arena.alloc("h2T", [128, NCH, TOWN], BF16)
        B_h2T = Buf("h2T")
        m_p4 = arena.mark()
        hold = norm_transpose_phase(out, TOWN // 128, norm2_g, h2T, B_h2T, TOWN, "p4", lag=2)
        pool_hold.extend(hold)
        if debug:
            P.dma("sp", lambda e: e.dma_start(out=dbg_h2.rearrange("c p t -> p c t"), in_=h2T[:]), B_h2T,
                  reads=[B_h2T])
        P.barrier(("pe", "act", "dve", "sp"))
        late_barrier = [(s_, s_.count) for s_ in P.prog_sem.values() if s_.count > 0]
        late_barrier += [(s_, s_.count) for s_ in P.all_dma_sems if s_.count > 0]
        arena.reset(m_p4)
    if upto >= 6:
        uT = arena.alloc("uT", [128, 64, 512], BF16)
        B_uT = Buf("uT")
        rt = [arena.alloc(f"rt{i}", [128, 512], F32) for i in range(2)]
        B_rt = [Buf(f"rt{i}") for i in range(2)]
        yo = [arena.alloc(f"yo{i}", [128, 512], F32) for i in range(4)]
        B_yo = [Buf(f"yo{i}") for i in range(4)]
        for i in range(2):
            wring.append(arena.alloc(f"wringx{i}", [128, 8192], BF16))
            wbuf.append(Buf(f"wringx{i}"))
        xctr = 0
        rctr = 0
        ff1_v = w_ff1.rearrange("(k p) n -> p k n", p=128)
        ff2_v = w_ff2.rearrange("(j p) m -> p j m", p=128)
        wctr[0] = 0
        nslab5 = [0]
        for tb in range(4):
            for sl in range(16):
                wt, wb = wslot()
                xw = []
                if nslab5[0] == 0:
                    xw = list(pool_hold)
                elif nslab5[0] == 2:
                    xw = list(late_barrier)
                nslab5[0] += 1
                P.dma("pool", lambda e, wt=wt, sl=sl: e.dma_start(out=wt[:], in_=w1c[sl]),
                      wb, writes=[wb], extra_waits=xw)
                for ci in range(4):
                    bi = next_bank(0, 8)
                    fns = [(lambda e, k=k, ci=ci, bi=bi, wt=wt, tb=tb: e.matmul(
                        psum[bi][:, :], lhsT=wt[:, 512 * k + 128 * ci:512 * k + 128 * ci + 128],
                        rhs=h2T[:, k, 512 * tb:512 * tb + 512], start=(k == 0), stop=(k == NCH - 1)))
                        for k in range(NCH)]
                    P.group("pe", fns, reads=[wb, B_h2T], writes=[bank[bi]])
                    rs_ = rctr % 2
                    rctr += 1
                    P.op("act", lambda e, bi=bi, rs_=rs_: e.activation(out=rt[rs_][:], in_=psum[bi][:, :],
                                                                      func=AF.Relu),
                         reads=[bank[bi]], writes=[B_rt[rs_]])
                    P.op("dve", lambda e, bi=bi, rs_=rs_, sl=sl, ci=ci: e.scalar_tensor_tensor(
                        out=uT[:, 4 * sl + ci, :], in0=psum[bi][:, :], scalar=0.0, in1=rt[rs_][:],
                        op0=ALU.max, op1=ALU.mult), reads=[bank[bi], B_rt[rs_]], writes=[B_uT])
            for ms in range(4):
                pb0 = 4 if (4 * tb + ms) % 2 == 0 else 0
                for tt in range(4):
                    r0 = 512 * tb + 128 * tt
                    P.dma("sp", lambda e, tt=tt, r0=r0, ms=ms: e.dma_start(
                        out=yo[tt][:], in_=out[r0:r0 + 128, 512 * ms:512 * ms + 512]), B_yo[tt],
                        writes=[B_yo[tt]])
                for jg in range(8):
                    wt, wb = wslot()
                    P.dma("pool", lambda e, wt=wt, jg=jg, ms=ms: e.dma_start(
                        out=wt[:, 0:4096], in_=w2c[2 * ms + jg // 4][:, 4096 * (jg % 4):4096 * (jg % 4) + 4096]),
                        wb, writes=[wb])
                    fns = []
                    for jj in range(8):
                        for tt in range(4):
                            fns.append(lambda e, jj=jj, tt=tt, jg=jg, wt=wt, pb0=pb0: e.matmul(
                                psum[pb0 + tt][:, :], lhsT=uT[:, 8 * jg + jj, 128 * tt:128 * tt + 128],
                                rhs=wt[:, 512 * jj:512 * jj + 512], start=(jg == 0 and jj == 0),
                                stop=(jg == 7 and jj == 7)))
                    P.group("pe", fns, reads=[wb, B_uT], writes=[bank[pb0 + t_] for t_ in range(4)])
                for tt in range(4):
                    r0 = 512 * tb + 128 * tt
                    P.op("dve", lambda e, tt=tt, pb0=pb0: e.tensor_tensor(out=yo[tt][:], in0=psum[pb0 + tt][:, :],
                                                                         in1=yo[tt][:], op=ALU.add),
                         reads=[bank[pb0 + tt]], writes=[B_yo[tt]])
                    P.dma("sp", lambda e, tt=tt, r0=r0, ms=ms: e.dma_start(
                        out=out[r0:r0 + 128, 512 * ms:512 * ms + 512], in_=yo[tt][:]), B_yo[tt],
                        reads=[B_yo[tt]])

    P.barrier(("sp",))

    with nc.Block() as block:
        @block.sync
        def _(e):
            P.replay("sp", e)

        @block.gpsimd
        def _(e):
            P.replay("pool", e)

        @block.scalar
        def _(e):
            P.replay("act", e)

        @block.vector
        def _(e):
            P.replay("dve", e)

        @block.tensor
        def _(e):
            P.replay("pe", e)
    return nc


def make_rtab(has_prev):
    kk = np.arange(128)[:, None].astype(np.float64)
    qq = np.arange(128)[None, :].astype(np.float64)
    cur = np.where(qq >= kk, -(qq - kk), NEG)
    prev_a = np.where(qq <= kk, -(128 + qq - kk), NEG)
    prev_b = np.where(qq < kk, -(128 + qq - kk), NEG)
    neg = np.full((128, 128), NEG)
    ha = prev_a if has_prev else neg
    hb = prev_b if has_prev else neg
    t = np.zeros((128, 6, 512), np.float32)
    t[:, 0] = np.tile(cur, (1, 4))
    t[:, 1] = np.tile(prev_a, (1, 4))
    t[:, 2] = np.tile(prev_b, (1, 4))
    t[:, 3] = np.concatenate([ha, prev_a, prev_a, prev_a], axis=1)
    t[:, 4] = np.tile(ha, (1, 4))
    t[:, 5] = np.concatenate([hb, prev_b, prev_b, prev_b], axis=1)
    return t


def make_in_maps(x, norm1_g, w_in, q_norm_a, k_norm_a, q_norm_b, k_norm_b, sinks_b, w_branch_a, w_branch_b,
                 w_out, norm2_g, w_ff1, w_ff2):
    f = lambda a: np.ascontiguousarray(np.asarray(a, dtype=np.float32))
    x = f(x)
    shared = {
        "w_in": f(w_in)[0], "norm1_g": f(norm1_g).reshape(1, D), "norm2_g": f(norm2_g).reshape(1, D),
        "gvec": np.ascontiguousarray(np.stack([f(q_norm_a)[0], f(k_norm_a)[0], f(q_norm_b)[0], f(k_norm_b)[0]],
                                              axis=1)),
        "sinks": f(sinks_b).reshape(1, 16),
        "w_bra": f(w_branch_a)[0], "w_brb": f(w_branch_b)[0], "w_out": f(w_out)[0],
        "w_ff1": f(w_ff1)[0], "w_ff2": f(w_ff2)[0],
        "ident": np.eye(128, dtype=np.float32).astype(ml_dtypes.bfloat16),
    }
    maps = []
    for c in range(N_CORES):
        b, s = divmod(c, 4)
        xe = np.zeros((TEXT, D), np.float32)
        xe[TOWN:] = x[b, s * TOWN:(s + 1) * TOWN]
        if s > 0:
            xe[:TOWN] = x[b, (s - 1) * TOWN:s * TOWN]
        m = dict(shared)
        m["xe"] = xe
        m["rtab"] = make_rtab(s > 0)
        maps.append(m)
    return maps


_CACHE = {}


def kernel(**inputs):
    if "nc" not in _CACHE:
        _CACHE["nc"] = build_program()
    nc = _CACHE["nc"]
    maps = make_in_maps(**inputs)
    res = run_bass_kernel_spmd(nc, maps, core_ids=list(range(N_CORES)))
    outp = np.zeros((2, 4 * TOWN, D), np.float32)
    for c in range(N_CORES):
        b, s = divmod(c, 4)
        outp[b, s * TOWN:(s + 1) * TOWN] = res.results[c]["out"]
    return outp
```
